# Optimizing a Trainium2 kernel written in Bass

```python
import jax, jax.numpy as jnp
from jax import lax
import numpy as np

D_MODEL = 1024
BATCH = 8
SEQ = 2048
DEPTH = 1
DEC_BATCH = 16
DEC_SEQ = 64
PAST_LEN = 2048

CHUNK = 64
D_A = D_MODEL
D_B = D_MODEL
CONV_A_WIDTH = 3
CONV_B_WIDTH = 31
D_FF = 2816
D_IN = 3 * D_A + 2 * D_B + 2 * D_MODEL
EPS = 1e-6

kernel_name = "hybrid_streaming_conv_encoder_step"


def rms_norm(x, g):
    xf = x.astype(jnp.float32)
    y = xf * lax.rsqrt(jnp.mean(xf * xf, axis=-1, keepdims=True) + EPS)
    return (y * g.astype(jnp.float32)).astype(x.dtype)


def layer_norm(x, g, b):
    xf = x.astype(jnp.float32)
    mu = jnp.mean(xf, axis=-1, keepdims=True)
    var = jnp.mean(jnp.square(xf - mu), axis=-1, keepdims=True)
    y = (xf - mu) * lax.rsqrt(var + EPS)
    return (y * g.astype(jnp.float32) + b.astype(jnp.float32)).astype(x.dtype)


def swiglu_ffn(x, w_gate_up, w_down):
    gate, up = jnp.split(x @ w_gate_up, 2, axis=-1)
    return (jax.nn.silu(gate) * up) @ w_down


def causal_depthwise_conv(x, hist, w):
    width = w.shape[0]
    xp = jnp.concatenate([hist.astype(x.dtype), x], axis=1)
    y = lax.conv_general_dilated(
        xp, w[:, None, :].astype(x.dtype), window_strides=(1,), padding="VALID",
        dimension_numbers=("NWC", "WIO", "NWC"), feature_group_count=x.shape[-1])
    new_hist = xp[:, xp.shape[1] - (width - 1):]
    return y, new_hist


def encoder_layer(x, hist_a, hist_b, ffn1_norm, ffn1_w_gate_up, ffn1_w_down, mix_norm, w_in,
                  conv_a_w, conv_b_w, conv_b_bias, conv_b_ln_g, conv_b_ln_b,
                  w_a_out, w_b_out, w_out, ffn2_norm, ffn2_w_gate_up, ffn2_w_down):
    x = x + 0.5 * swiglu_ffn(rms_norm(x, ffn1_norm), ffn1_w_gate_up, ffn1_w_down)
    h = rms_norm(x, mix_norm)
    p = h @ w_in
    cuts = [D_A, 2 * D_A, 3 * D_A, 3 * D_A + 2 * D_B, 3 * D_A + 2 * D_B + D_MODEL]
    a_b, a_c, a_v, b_u, gate_a, gate_b = jnp.split(p, cuts, axis=-1)
    a_conv, new_a = causal_depthwise_conv(a_c * a_v, hist_a, conv_a_w)
    y_a = (a_b * a_conv) @ w_a_out
    glu = b_u[..., :D_B] * jax.nn.sigmoid(b_u[..., D_B:])
    b_conv, new_b = causal_depthwise_conv(glu, hist_b, conv_b_w)
    b_act = jax.nn.silu(layer_norm(b_conv + conv_b_bias, conv_b_ln_g, conv_b_ln_b))
    y_b = b_act @ w_b_out
    merged = jax.nn.sigmoid(gate_a) * y_a + jax.nn.sigmoid(gate_b) * y_b
    x = x + merged @ w_out
    x = x + 0.5 * swiglu_ffn(rms_norm(x, ffn2_norm), ffn2_w_gate_up, ffn2_w_down)
    return x, new_a, new_b


def setup_inputs(seed: int = 0) -> dict:
    key = jax.random.key(seed)
    k = jax.random.split(key, 24)
    f32 = jnp.float32

    def nrm(kk, shape, scale):
        return jax.random.normal(kk, shape, f32) * scale

    def gain(kk, shape):
        return 1.0 + 0.02 * jax.random.normal(kk, shape, f32)

    L = DEPTH
    return {
        "x_prompt": nrm(k[0], (BATCH, SEQ, D_MODEL), 1.0),
        "x_sample": nrm(k[1], (DEC_BATCH, DEC_SEQ, D_MODEL), 1.0),
        "cache_conv_a": nrm(k[2], (L, DEC_BATCH, CONV_A_WIDTH - 1, D_A), 1.0),
        "cache_conv_b": nrm(k[3], (L, DEC_BATCH, CONV_B_WIDTH - 1, D_B), 1.0),
        "ffn1_norm": gain(k[4], (L, D_MODEL)),
        "ffn1_w_gate_up": nrm(k[5], (L, D_MODEL, 2 * D_FF), D_MODEL ** -0.5),
        "ffn1_w_down": nrm(k[6], (L, D_FF, D_MODEL), D_FF ** -0.5),
        "mix_norm": gain(k[7], (L, D_MODEL)),
        "w_in": nrm(k[8], (L, D_MODEL, D_IN), D_MODEL ** -0.5),
        "conv_a_w": nrm(k[9], (L, CONV_A_WIDTH, D_A), CONV_A_WIDTH ** -0.5),
        "conv_b_w": nrm(k[10], (L, CONV_B_WIDTH, D_B), CONV_B_WIDTH ** -0.5),
        "conv_b_bias": nrm(k[11], (L, D_B), 0.02),
        "conv_b_ln_g": gain(k[12], (L, D_B)),
        "conv_b_ln_b": nrm(k[13], (L, D_B), 0.02),
        "w_a_out": nrm(k[14], (L, D_A, D_MODEL), D_A ** -0.5),
        "w_b_out": nrm(k[15], (L, D_B, D_MODEL), D_B ** -0.5),
        "w_out": nrm(k[16], (L, D_MODEL, D_MODEL), D_MODEL ** -0.5),
        "ffn2_norm": gain(k[17], (L, D_MODEL)),
        "ffn2_w_gate_up": nrm(k[18], (L, D_MODEL, 2 * D_FF), D_MODEL ** -0.5),
        "ffn2_w_down": nrm(k[19], (L, D_FF, D_MODEL), D_FF ** -0.5),
        "final_norm": gain(k[20], (D_MODEL,)),
    }


def reference(x_prompt, x_sample, cache_conv_a, cache_conv_b, ffn1_norm, ffn1_w_gate_up,
              ffn1_w_down, mix_norm, w_in, conv_a_w, conv_b_w, conv_b_bias, conv_b_ln_g,
              conv_b_ln_b, w_a_out, w_b_out, w_out, ffn2_norm, ffn2_w_gate_up, ffn2_w_down,
              final_norm):
    xp, xs = x_prompt, x_sample
    pa, pb, sa, sb = [], [], [], []
    for l in range(DEPTH):
        w = (ffn1_norm[l], ffn1_w_gate_up[l], ffn1_w_down[l], mix_norm[l], w_in[l],
             conv_a_w[l], conv_b_w[l], conv_b_bias[l], conv_b_ln_g[l], conv_b_ln_b[l],
             w_a_out[l], w_b_out[l], w_out[l], ffn2_norm[l], ffn2_w_gate_up[l], ffn2_w_down[l])
        zero_a = jnp.zeros((xp.shape[0], CONV_A_WIDTH - 1, D_A), xp.dtype)
        zero_b = jnp.zeros((xp.shape[0], CONV_B_WIDTH - 1, D_B), xp.dtype)
        xp, hpa, hpb = encoder_layer(xp, zero_a, zero_b, *w)
        xs, hsa, hsb = encoder_layer(xs, cache_conv_a[l], cache_conv_b[l], *w)
        pa.append(hpa); pb.append(hpb); sa.append(hsa); sb.append(hsb)
    y_prompt = rms_norm(xp, final_norm)
    y_sample = rms_norm(xs, final_norm)
    new_conv_a_prompt = jnp.stack(pa, axis=0)
    new_conv_b_prompt = jnp.stack(pb, axis=0)
    new_conv_a_sample = jnp.stack(sa, axis=0)
    new_conv_b_sample = jnp.stack(sb, axis=0)
    return (y_prompt, y_sample, new_conv_a_prompt, new_conv_b_prompt, new_conv_a_sample, new_conv_b_sample)
```

```python
import numpy as np
import concourse.bass as bass
import concourse.mybir as mybir
from concourse.bass_utils import run_bass_kernel_spmd

F32 = mybir.dt.float32
F32R = mybir.dt.float32r
ALU = mybir.AluOpType
AF = mybir.ActivationFunctionType

D = 1024
DFF = 2816
NCH = 8
NJ = 22
TS = 544
NT = 272
NST = 4
HB = 30
HA = 2
EPS = 1e-6
GXW = 432
CXW = 344
NDIAG = 16
NSLOT = 4
XPOSE_MM = True
KMAJOR = True
LN_SPLIT = True
DIAG_ENG = "dve"
CG = 8
R_FFN1, R_MIX, R_FFN2, R_FIN, R_CBB, R_LNG, R_LNB, R_CAW, R_CBW = 0, 1, 2, 3, 4, 5, 6, 7, 10
NCR = 41


class Buf:
    __slots__ = ("lw", "rd")

    def __init__(self):
        self.lw = None
        self.rd = []


class Prog:
    ENG = ("pe", "act", "dve", "pool", "sp")

    def __init__(self):
        self.streams = {e: [] for e in self.ENG}
        self.cnt = {e: 0 for e in self.ENG}
        self.waited = {e: {} for e in self.ENG}
        self.dcnt = {}
        self.sems = {}

    def _deps(self, engine, reads, writes):
        need = {}
        for b in reads:
            if b.lw is not None:
                k, v = b.lw
                need[k] = max(need.get(k, 0), v)
        for b in writes:
            if b.lw is not None:
                k, v = b.lw
                need[k] = max(need.get(k, 0), v)
            for (k, v) in b.rd:
                need[k] = max(need.get(k, 0), v)
        w = self.waited[engine]
        waits = []
        if engine == "pe":
            need.pop("pe", None)
        for k, v in need.items():
            if w.get(k, 0) < v:
                waits.append((k, v))
                w[k] = v
        return waits

    def _mark(self, tok, reads, writes):
        for b in reads:
            b.rd.append(tok)
        for b in writes:
            b.lw = tok
            b.rd = []

    def op(self, engine, fn, reads=(), writes=()):
        waits = self._deps(engine, reads, writes)
        self.cnt[engine] += 1
        tok = (engine, self.cnt[engine])
        sems = self.sems

        def emit(eng):
            for (k, v) in waits:
                eng.wait_ge(sems[k], v)
            ins = fn(eng)
            ins.then_inc(sems[engine], 1)
        self.streams[engine].append(emit)
        self._mark(tok, reads, writes)
        return tok

    def dma(self, queue, dsem, out_ap, in_ap, reads=(), writes=(), cont=False):
        waits = self._deps(queue, reads, writes)
        if cont:
            waits = [(k, v) for (k, v) in waits if k != dsem]
        self.dcnt[dsem] = self.dcnt.get(dsem, 0) + 16
        tok = (dsem, self.dcnt[dsem])
        sems = self.sems

        def emit(eng):
            for (k, v) in waits:
                eng.wait_ge(sems[k], v)
            eng.dma_start(out=out_ap, in_=in_ap).then_inc(sems[dsem], 16)
        self.streams[queue].append(emit)
        self._mark(tok, reads, writes)
        return tok

    def wait_all(self, engine, toks):
        need = {}
        for (k, v) in toks:
            need[k] = max(need.get(k, 0), v)
        sems = self.sems
        lst = list(need.items())

        def emit(eng):
            for (k, v) in lst:
                eng.wait_ge(sems[k], v)
        self.streams[engine].append(emit)


def st_segments(s):
    if s < 3:
        return [("P", TS * s, TS, 0)]
    return [("P", 1632, 416, 0), ("S0", 0, 64, 416), ("S1", 0, 64, 480)]


def tile_pieces(s, t):
    out = []
    lo, hi = NT * t, NT * (t + 1)
    for (kind, r0, ln, c0) in st_segments(s):
        a, b = max(lo, c0), min(hi, c0 + ln)
        if a < b:
            out.append((kind, r0 + (a - c0), b - a, a - lo))
    return out


def ext_layout(pieces, H):
    starts = []
    pos = 0
    for (_, _, ln, _) in pieces:
        starts.append(pos + H)
        pos += H + ln
    return starts, pos


def io_blocks(s):
    out = []
    for (kind, r0, ln, c0) in st_segments(s):
        o = 0
        while o < ln:
            nb = min(128, ln - o)
            out.append((kind, r0 + o, nb, c0 + o))
            o += nb
    return out


def build_nc(dbg=None):
    nc = bass.Bass("TRN2", target_bir_lowering=False)

    def din(name, shape):
        return nc.dram_tensor(name, shape, F32, kind="ExternalInput").ap()

    def dout(name, shape):
        return nc.dram_tensor(name, shape, F32, kind="ExternalOutput").ap()

    x_p = din("x_p", [2048, D])
    x_s = din("x_s", [128, D])
    hist_in = din("hist", [64, D])
    cst_in = din("cst", [NCR, D])
    ident_in = din("ident_in", [128, 128])
    w_gu = [din("w_gu1", [D, 2 * DFF]), din("w_gu2", [D, 2 * DFF])]
    w_dn = [din("w_dn1", [DFF, D]), din("w_dn2", [DFF, D])]
    w_in = din("w_in", [D, 7 * D])
    w_ao = din("w_ao", [D, D])
    w_bo = din("w_bo", [D, D])
    w_o = din("w_o", [D, D])
    y_p = dout("y_p", [2048, D])
    y_s = dout("y_s", [128, D])
    nca_p = dout("nca_p", [2, D])
    ncb_p = dout("ncb_p", [30, D])
    nca_s = dout("nca_s", [4, D])
    ncb_s = dout("ncb_s", [60, D])
    if dbg is not None:
        dbg_x = dout("dbg_x", [128, NCH * TS])
        dbg_h = dout("dbg_h", [128, NCH * TS])
        dbg_a = dout("dbg_a", [128, 24 * TS])

    def wv(w):
        return w.rearrange("(kb p) n -> p kb n", p=128)

    w_gu_v = [wv(w) for w in w_gu]
    w_dn_v = [wv(w) for w in w_dn]
    w_in_v = wv(w_in)
    w_ao_v, w_bo_v, w_o_v = wv(w_ao), wv(w_bo), wv(w_o)

    P = Prog()
    import contextlib
    with contextlib.ExitStack() as es:
        def sb(name, shape, dt=F32):
            return es.enter_context(nc.sbuf_tensor(name, shape, dt))

        def sem(name):
            s_ = es.enter_context(nc.semaphore(name))
            P.sems[name] = s_
            return name

        for e in Prog.ENG:
            sem(e)

        xT = sb("xT", [128, NCH, TS])
        hT = sb("hT", [128, NCH, TS])
        arena = sb("arena", [128, 24, TS])
        slots = [sb("wslot%d" % i, [128, 4096], F32R) for i in range(NSLOT)]
        slot_sem = [sem("wsl%d" % i) for i in range(NSLOT)]
        slot_buf = [Buf() for _ in range(NSLOT)]
        diag = sb("diag", [128, NDIAG, 128], F32R)
        diag_buf = [Buf() for _ in range(NDIAG)]
        diagA = sb("diagA", [128, 6, 128], F32R)
        diagA_buf = [Buf() for _ in range(6)]
        gx = [[sb("gx%d%d" % (a, b), [128, GXW]) for b in range(2)] for a in range(2)]
        gx_buf = [[Buf() for _ in range(2)] for _ in range(2)]
        cx = [[sb("cx%d%d" % (a, b), [128, CXW]) for b in range(2)] for a in range(2)]
        cx_buf = [[Buf() for _ in range(2)] for _ in range(2)]
        NTA, NTD, NSTT = 6, 4, 6
        tmpA_all = sb("tmpA_all", [128, NTA, NT])
        tmpA = [tmpA_all[:, i, :] for i in range(NTA)]
        tmpA_buf = [Buf() for _ in range(NTA)]
        tmpD_all = sb("tmpD_all", [128, NTD, NT])
        tmpD = [tmpD_all[:, i, :] for i in range(NTD)]
        tmpD_buf = [Buf() for _ in range(NTD)]
        NTQ = 3
        tmpQ = [sb("tmpQ%d" % i, [128, NT], F32R) for i in range(NTQ)]
        tmpQ_buf = [Buf() for _ in range(NTQ)]
        stt_all = sb("stt_all", [128, NSTT, NT])
        stt = [stt_all[:, i, :] for i in range(NSTT)]
        stt_buf = [Buf() for _ in range(NSTT)]
        stage = [sb("stage%d" % i, [128, D]) for i in range(2)]
        stage_buf = [Buf() for _ in range(2)]
        stage_ld = [sem("stld%d" % i) for i in range(2)]
        LD = [tmpA_all[:, 0:4, :].rearrange("p a n -> p (a n)"), tmpD_all[:, 0:4, :].rearrange("p a n -> p (a n)"),
              stt_all[:, 0:4, :].rearrange("p a n -> p (a n)")]
        LD_buf = [tmpA_buf[0:4], tmpD_buf[0:4], stt_buf[0:4]]
        ld_sem = [sem("xld%d" % i) for i in range(3)]
        ld_sem_sw = [sem("xlds%d" % i) for i in range(3)]
        stage_st = [sem("stst%d" % i) for i in range(2)]
        ident = sb("ident", [128, 128])
        ones = sb("ones", [128, 128], F32R)
        ones_f = sb("ones_f", [128, 128])
        c_onesf = Buf()
        epst = sb("epst", [128, 1])
        dmy = sb("dmy", [128, 1])
        c_dmy = Buf()
        cvec = sb("cvec", [128, NCH, NCR])
        hist = sb("hist_sb", [128, NCH, 64])
        carryB = sb("carryB", [128, NCH, HB])
        carryA = sb("carryA", [128, NCH, HA])
        gatB = sb("gatB", [128, 3 * HB])
        gatA = sb("gatA", [128, 3 * HA])
        c_ident, c_ones, c_eps, c_cvec, c_hist = Buf(), Buf(), Buf(), Buf(), Buf()
        c_carryB = [Buf() for _ in range(NCH)]
        c_carryA = [Buf() for _ in range(NCH)]
        c_gatB, c_gatA = Buf(), Buf()
        psum = [es.enter_context(nc.psum_tensor("ps%d" % i, [128, 512], F32)) for i in range(8)]
        psum_buf = [Buf() for _ in range(8)]
        x_buf = [[Buf() for _ in range(2)] for _ in range(NCH)]
        h_buf = [[Buf() for _ in range(2)] for _ in range(NCH)]
        a_buf = [[Buf() for _ in range(2)] for _ in range(24)]
        misc_sem = sem("misc")
        out_toks = []

        state = {"bank": 0, "diagA": 0, "tq": 0, "ta": 0, "td": 0, "st": 0, "slot": 0, "diag": 0, "stage": 0}

        def rr(key, n):
            v = state[key]
            state[key] = (v + 1) % n
            return v

        def bank():
            return rr("bank", 8)

        def tA():
            return rr("ta", NTA)

        def tD():
            return rr("td", NTD)

        def tQ():
            return rr("tq", NTQ)

        def tS():
            return rr("st", NSTT)

        def r_(ap):
            return ap.bitcast(F32R)

        def tcols(t, a=0, b=NT):
            return slice(NT * t + a, NT * t + b)

        def mm_group(out_ap, pairs, reads, wbuf, first=True, last=True):
            n = len(pairs)

            def fn(eng):
                ins = None
                for i, (l, r) in enumerate(pairs):
                    ins = eng.matmul(out_ap, lhsT=l, rhs=r, start=(first and i == 0),
                                     stop=(last and i == n - 1))
                return ins
            return P.op("pe", fn, reads=reads, writes=[wbuf])

        def act_op(out_ap, in_ap, func, reads, writes, bias=None, scale=None):
            def fn(eng):
                kw = {}
                if bias is not None:
                    kw["bias"] = bias
                if scale is not None:
                    kw["scale"] = scale
                return eng.activation(out=out_ap, in_=in_ap, func=func, **kw)
            return P.op("act", fn, reads=reads, writes=writes)

        def dummy_act(func):
            return act_op(dmy[:, 0:1], epst[:, 0:1], func, reads=[c_eps], writes=[c_dmy])

        def tt_op(out_ap, in0, in1, op, reads, writes, engine="dve"):
            def fn(eng):
                return eng.tensor_tensor(out=out_ap, in0=in0, in1=in1, op=op)
            return P.op(engine, fn, reads=reads, writes=writes)

        def stt_op(out_ap, in0, scalar, in1, op0, op1, reads, writes):
            def fn(eng):
                return eng.scalar_tensor_tensor(out=out_ap, in0=in0, scalar=scalar, in1=in1, op0=op0, op1=op1)
            return P.op("dve", fn, reads=reads, writes=writes)

        def ts_op(out_ap, in0, s1, s2, op0, op1, reads, writes, engine="dve"):
            def fn(eng):
                return eng.tensor_scalar(out=out_ap, in0=in0, scalar1=s1, scalar2=s2, op0=op0, op1=op1)
            return P.op(engine, fn, reads=reads, writes=writes)

        def copy_op(engine, out_ap, in_ap, reads, writes):
            def fn(eng):
                if engine == "act":
                    return eng.copy(out=out_ap, in_=in_ap)
                return eng.tensor_copy(out=out_ap, in_=in_ap)
            return P.op(engine, fn, reads=reads, writes=writes)

        def transpose_group(items, reads, wbuf):
            def fn(eng):
                ins = None
                for (o, i_, idn) in items:
                    if XPOSE_MM:
                        ins = eng.matmul(o, lhsT=i_, rhs=idn, start=True, stop=True)
                    else:
                        ins = eng.transpose(out=o, in_=i_, identity=idn)
                return ins
            return P.op("pe", fn, reads=reads, writes=[wbuf])

        units = []
        for s in range(NST):
            for which in range(2):
                if which == 1:
                    for cc in range(4):
                        units.append(("CV", [(w_in_v, 0, 8, 1024 + cc * 256), (w_in_v, 0, 8, 2048 + cc * 256)]))
                        units.append(("AB", [(w_in_v, 0, 8, cc * 256)]))
                        units.append(("UU", [(w_in_v, 0, 8, 3072 + cc * 256), (w_in_v, 0, 8, 4096 + cc * 256)]))
                    for ff in range(4):
                        units.append(("GG", [(w_in_v, 0, 8, 5120 + ff * 256), (w_in_v, 0, 8, 6144 + ff * 256)]))
                        units.append(("AO", [(w_ao_v, 0, 8, ff * 256), (w_bo_v, 0, 8, ff * 256)]))
                    for fp in range(2):
                        units.append(("WO", [(w_o_v, 0, 8, fp * 512), (w_o_v, 0, 8, fp * 512 + 256)]))
                for jj in range(11):
                    units.append(("GU", [(w_gu_v[which], 0, 8, jj * 256), (w_gu_v[which], 0, 8, DFF + jj * 256)]))
                for ff in range(4):
                    for half in range(2):
                        units.append(("DN", [(w_dn_v[which], 11 * half, 11, ff * 256)]))
        ustate = {"next": 0, "issued": 0}

        def issue_load(u):
            tag, halves = units[u]
            sl = u % NSLOT
            for hi, (wview, kb0, nkb, col0) in enumerate(halves):
                off = hi * 2048
                o = slots[sl][:, off:off + nkb * 256].rearrange("p (k n) -> p k n", n=256)
                i_ = wview[:, kb0:kb0 + nkb, col0:col0 + 256]
                P.dma("pool", slot_sem[sl], o, i_, writes=[slot_buf[sl]], cont=(hi > 0))

        def next_unit(tag, hold=0):
            u = ustate["next"]
            assert units[u][0] == tag, (units[u][0], tag)
            while ustate["issued"] < min(len(units), u + NSLOT - hold):
                issue_load(ustate["issued"])
                ustate["issued"] += 1
            ustate["next"] = u + 1
            return u % NSLOT

        def slot4(sl):
            return slots[sl][:, :].rearrange("p (h k n) -> p h k n", h=2, k=8, n=256)

        def slot_half(sl, hi):
            return slots[sl][:, hi * 2048:(hi + 1) * 2048].rearrange("p (k n) -> p k n", n=256)

        P.dma("sp", misc_sem, ident[:, :], ident_in, writes=[c_ident])
        P.op("pool", lambda eng: eng.memset(ones_f[:, :], 1.0), writes=[c_onesf])
        copy_op("pool", ones[:, :], ones_f[:, :], reads=[c_onesf], writes=[c_ones])
        P.op("pool", lambda eng: eng.memset(epst[:, :], EPS), writes=[c_eps])
        P.op("pool", lambda eng: eng.memset(carryB[:, :, :], 0.0), writes=c_carryB)
        P.op("pool", lambda eng: eng.memset(carryA[:, :, :], 0.0), writes=c_carryA)
        P.dma("sp", stage_ld[0], stage[0][0:NCR, :], cst_in, writes=[stage_buf[0]])
        P.dma("sp", stage_ld[1], stage[1][0:64, :], hist_in, writes=[stage_buf[1]])
        b = bank()
        transpose_group([(psum[b][:, c * 64:c * 64 + NCR], stage[0][0:NCR, c * 128:(c + 1) * 128], ident[0:NCR, 0:NCR])
                         for c in range(NCH)], reads=[stage_buf[0], c_ident], wbuf=psum_buf[b])
        copy_op("dve", cvec[:, :, :], psum[b][:, :].rearrange("p (c w) -> p c w", w=64)[:, :, 0:NCR],
                reads=[psum_buf[b]], writes=[c_cvec])
        b = bank()
        transpose_group([(psum[b][:, c * 64:(c + 1) * 64], stage[1][0:64, c * 128:(c + 1) * 128], ident[0:64, 0:64])
                         for c in range(NCH)], reads=[stage_buf[1], c_ident], wbuf=psum_buf[b])
        copy_op("act", hist[:, :, :], psum[b][:, :].rearrange("p (c w) -> p c w", w=64),
                reads=[psum_buf[b]], writes=[c_hist])

        def tiles_of(c0, nb):
            return [t for t in range(2) if c0 < NT * (t + 1) and c0 + nb > NT * t]

        def x_src(kind, r0, nb):
            if kind == "P":
                return x_p[r0:r0 + nb, :]
            o = 0 if kind == "S0" else 64
            return x_s[o + r0:o + r0 + nb, :]

        def prefetch_x(s, ks=(0, 1)):
            blocks = io_blocks(s)
            for k in ks:
                (kind, r0, nb, c0) = blocks[k]
                P.dma("sp", ld_sem[k % 3], LD[k % 3][0:nb, 0:D], x_src(kind, r0, nb), writes=LD_buf[k % 3])

        def load_x(s, prefetched=()):
            for k, (kind, r0, nb, c0) in enumerate(io_blocks(s)):
                sg = k % 3
                if k not in prefetched:
                    q = "pool" if prefetched else "sp"
                    P.dma(q, (ld_sem_sw if q == "pool" else ld_sem)[sg], LD[sg][0:nb, 0:D], x_src(kind, r0, nb),
                          writes=LD_buf[sg])
                for g in range(2):
                    b = bank()
                    transpose_group([(psum[b][:, i * 128:i * 128 + nb],
                                      LD[sg][0:nb, (4 * g + i) * 128:(4 * g + i + 1) * 128],
                                      ident[0:nb, 0:nb]) for i in range(4)],
                                    reads=LD_buf[sg] + [c_ident], wbuf=psum_buf[b])
                    wb = [x_buf[4 * g + i][t] for i in range(4) for t in tiles_of(c0, nb)]
                    copy_op("act" if g == 0 else "dve", xT[:, 4 * g:4 * g + 4, c0:c0 + nb],
                            psum[b][:, :].rearrange("p (i w) -> p i w", w=128)[:, :, 0:nb],
                            reads=[psum_buf[b]], writes=wb)

        def rms_norm(row, inplace, nxt=None):
            for t in range(2):
                b = bank()
                for f in range(NCH):
                    i = tQ()
                    if True:
                        act_op(tmpQ[i][:, :], xT[:, f, tcols(t)], AF.Square,
                               reads=[x_buf[f][t]], writes=[tmpQ_buf[i]])
                    else:
                        tt_op(tmpQ[i][:, :], xT[:, f, tcols(t)], xT[:, f, tcols(t)], ALU.mult,
                              reads=[x_buf[f][t]], writes=[tmpQ_buf[i]])
                    mm_group(psum[b][:, 0:NT], [(ones[:, :], tmpQ[i][:, :])],
                             reads=[tmpQ_buf[i], c_ones], wbuf=psum_buf[b], first=(f == 0), last=(f == NCH - 1))
                i1, i2 = tS(), tS()
                act_op(stt[i1][:, :], psum[b][:, 0:NT], AF.Sqrt, reads=[psum_buf[b], c_eps], writes=[stt_buf[i1]],
                       bias=epst[:, 0:1], scale=1.0 / D)
                if t == 1 and nxt is not None:
                    dummy_act(nxt)

                def fn(eng, i1=i1, i2=i2):
                    return eng.reciprocal(out=stt[i2][:, :], in_=stt[i1][:, :])
                P.op("dve", fn, reads=[stt_buf[i1]], writes=[stt_buf[i2]])
                for f in range(NCH):
                    if inplace:
                        o, wb = xT[:, f, tcols(t)], [x_buf[f][t]]
                    else:
                        o, wb = r_(hT[:, f, tcols(t)]), [h_buf[f][t]]
                    stt_op(o, xT[:, f, tcols(t)], cvec[:, f, row:row + 1], stt[i2][:, :], ALU.mult, ALU.mult,
                           reads=[x_buf[f][t], stt_buf[i2], c_cvec], writes=wb)

        def kmajor_first(w4, sl, t):
            bk = {(hf, sub): bank() for sub in range(2) for hf in range(2)}
            for kb in range(NCH):
                def fn(eng, kb=kb, bk=bk, t=t):
                    ins = None
                    for sub in range(2):
                        for hf in range(2):
                            ins = eng.matmul(psum[bk[(hf, sub)]][:, 0:NT], lhsT=w4[:, hf, kb, sub * 128:(sub + 1) * 128],
                                             rhs=r_(hT[:, kb, tcols(t)]), start=(kb == 0), stop=(kb == NCH - 1))
                    return ins
                P.op("pe", fn, reads=[slot_buf[sl], h_buf[kb][t]], writes=[psum_buf[b_] for b_ in bk.values()])
            return bk

        def ffn(which, after_gu=None):
            for jj in range(11):
                sl = next_unit("GU")
                w4 = slot4(sl)
                combos = [(sub, t) for sub in range(2) for t in range(2)]
                if jj == 0:
                    combos = [(sub, t) for t in range(2) for sub in range(2)]
                pre = {}
                for (sub, t) in combos:
                    j = 2 * jj + sub
                    if jj == 0 and KMAJOR:
                        if t not in pre:
                            pre[t] = kmajor_first(w4, sl, t)
                        bG, bU = pre[t][(0, sub)], pre[t][(1, sub)]
                    else:
                        bG, bU = bank(), bank()
                        hr = [h_buf[kb][t] for kb in range(NCH)]
                        mm_group(psum[bG][:, 0:NT], [(w4[:, 0, kb, sub * 128:(sub + 1) * 128], r_(hT[:, kb, tcols(t)]))
                                                     for kb in range(NCH)], reads=[slot_buf[sl]] + hr, wbuf=psum_buf[bG])
                        mm_group(psum[bU][:, 0:NT], [(w4[:, 1, kb, sub * 128:(sub + 1) * 128], r_(hT[:, kb, tcols(t)]))
                                                     for kb in range(NCH)], reads=[slot_buf[sl]] + hr, wbuf=psum_buf[bU])
                    if True:
                        i = tA()
                        act_op(tmpA[i][:, 0:NT], psum[bG][:, 0:NT], AF.Silu, reads=[psum_buf[bG]], writes=[tmpA_buf[i]])
                        tt_op(r_(arena[:, j, tcols(t)]), psum[bU][:, 0:NT], tmpA[i][:, 0:NT], ALU.mult,
                              reads=[psum_buf[bU], tmpA_buf[i]], writes=[a_buf[j][t]])
            dummy_act(AF.Sqrt)
            if after_gu is not None:
                after_gu()
            for ff in range(4):
                bks = [[bank() for t in range(2)] for sub in range(2)]
                for half in range(2):
                    sl = next_unit("DN")
                    wd = slots[sl][:, 0:11 * 256].rearrange("p (k n) -> p k n", n=256)
                    for sub in range(2):
                        for t in range(2):
                            bO = bks[sub][t]
                            mm_group(psum[bO][:, 0:NT],
                                     [(wd[:, kk, sub * 128:(sub + 1) * 128], r_(arena[:, 11 * half + kk, tcols(t)]))
                                      for kk in range(11)],
                                     reads=[slot_buf[sl]] + [a_buf[11 * half + kk][t] for kk in range(11)],
                                     wbuf=psum_buf[bO], first=(half == 0), last=(half == 1))
                for sub in range(2):
                    f = 2 * ff + sub
                    for t in range(2):
                        bO = bks[sub][t]
                        stt_op(xT[:, f, tcols(t)], psum[bO][:, 0:NT], 0.5, xT[:, f, tcols(t)], ALU.mult, ALU.add,
                               reads=[psum_buf[bO], x_buf[f][t]], writes=[x_buf[f][t]])

        def build_diag(c, row):
            d = rr("diag", NDIAG)
            ts_op(r_(diag[:, d, :]), ident[:, :], cvec[:, c, row:row + 1], 0.0, ALU.mult, ALU.add,
                  reads=[c_ident, c_cvec, ], writes=[diag_buf[d]], engine=DIAG_ENG)
            return d

        def build_diagA(c, row):
            d = rr("diagA", 6)
            ts_op(r_(diagA[:, d, :]), ident[:, :], cvec[:, c, row:row + 1], 0.0, ALU.mult, ALU.add,
                  reads=[c_ident, c_cvec, ], writes=[diagA_buf[d]], engine=DIAG_ENG)
            return d

        def halo_src(s, t, sub, c, kind, row0, which):
            H = HB if which == "B" else HA
            if kind == "P":
                if t == 0:
                    cr = carryB if which == "B" else carryA
                    cb_ = c_carryB if which == "B" else c_carryA
                    return cr[:, c, :], [cb_[c]]
                ext = gx if which == "B" else cx
                eb = gx_buf if which == "B" else cx_buf
                return ext[sub][0][:, NT:NT + H], [eb[sub][0]]
            if which == "B":
                o = 0 if kind == "S0" else 30
                return hist[:, c, o:o + 30], [c_hist]
            o = 60 if kind == "S0" else 62
            return hist[:, c, o:o + 2], [c_hist]

        def mixer(s):
            pcs = [tile_pieces(s, t) for t in range(2)]
            layA = [ext_layout(pcs[t], HA) for t in range(2)]
            layB = [ext_layout(pcs[t], HB) for t in range(2)]
            pend = {}

            def conv_b_pre(cc):
                pend[cc] = [build_diag(2 * cc, R_CBW + k) for k in range(CG)]

            def conv_b(cc):
                for sub in range(2):
                    c = 2 * cc + sub
                    bb = [bank(), bank()]
                    for k0 in range(0, HB + 1, CG):
                        ks = list(range(k0, min(HB + 1, k0 + CG)))
                        if sub == 0 and k0 == 0 and cc in pend:
                            ds = pend.pop(cc)
                        else:
                            ds = [build_diag(c, R_CBW + k) for k in ks]
                        for t in range(2):
                            nconv = layB[t][1] - HB
                            mm_group(psum[bb[t]][:, 0:nconv],
                                     [(diag[:, ds[i], :], r_(gx[sub][t][:, k:k + nconv])) for i, k in enumerate(ks)],
                                     reads=[diag_buf[d_] for d_ in ds] + [gx_buf[sub][t]], wbuf=psum_buf[bb[t]],
                                     first=(k0 == 0), last=(ks[-1] == HB))
                    for t in range(2):
                        for pi, (kind, row0, ln, tc0) in enumerate(pcs[t]):
                            es_ = layB[t][0][pi]
                            act_op(r_(arena[:, 8 + c, tcols(t, tc0, tc0 + ln)]), psum[bb[t]][:, es_ - HB:es_ - HB + ln],
                                   AF.Identity, reads=[psum_buf[bb[t]], c_cvec], writes=[a_buf[8 + c][t]],
                                   bias=cvec[:, c, R_CBB:R_CBB + 1])

            for cc in range(4):
                if cc > 0:
                    conv_b_pre(cc - 1)
                dsA = [[build_diagA(2 * cc + sub, R_CAW + k) for k in range(HA + 1)] for sub in range(2)]
                sl = next_unit("CV")
                w4 = slot4(sl)
                combos = [(sub, t) for sub in range(2) for t in range(2)]
                if cc == 0:
                    combos = [(sub, t) for t in range(2) for sub in range(2)]
                pre = {}
                for (sub, t) in combos:
                    c = 2 * cc + sub
                    if cc == 0 and KMAJOR:
                        if t not in pre:
                            pre[t] = kmajor_first(w4, sl, t)
                        bC, bV = pre[t][(0, sub)], pre[t][(1, sub)]
                    else:
                        bC, bV = bank(), bank()
                        hr = [h_buf[kb][t] for kb in range(NCH)]
                        mm_group(psum[bC][:, 0:NT], [(w4[:, 0, kb, sub * 128:(sub + 1) * 128], r_(hT[:, kb, tcols(t)]))
                                                     for kb in range(NCH)], reads=[slot_buf[sl]] + hr, wbuf=psum_buf[bC])
                        mm_group(psum[bV][:, 0:NT], [(w4[:, 1, kb, sub * 128:(sub + 1) * 128], r_(hT[:, kb, tcols(t)]))
                                                     for kb in range(NCH)], reads=[slot_buf[sl]] + hr, wbuf=psum_buf[bV])
                    if True:
                        i = tA()
                        copy_op("act", tmpA[i][:, 0:NT], psum[bC][:, 0:NT], reads=[psum_buf[bC]], writes=[tmpA_buf[i]])
                        for pi, (kind, row0, ln, tc0) in enumerate(pcs[t]):
                            es_ = layA[t][0][pi]
                            src, rb = halo_src(s, t, sub, c, kind, row0, "A")
                            copy_op("pool", r_(cx[sub][t][:, es_ - HA:es_]), src, reads=rb, writes=[cx_buf[sub][t]])
                            tt_op(r_(cx[sub][t][:, es_:es_ + ln]), psum[bV][:, tc0:tc0 + ln], tmpA[i][:, tc0:tc0 + ln],
                                  ALU.mult, reads=[psum_buf[bV], tmpA_buf[i]], writes=[cx_buf[sub][t]])
                        if t == 1 and s < NST - 1:
                            copy_op("pool", carryA[:, c, :], cx[sub][1][:, NT:NT + HA],
                                    reads=[cx_buf[sub][1]], writes=[c_carryA[c]])
                        if t == 1 and s == NST - 1:
                            copy_op("pool", gatA[:, :].rearrange("p (s w) -> p s w", w=HA),
                                    cx[sub][1][:, 144:144 + 3 * 66].rearrange("p (s w) -> p s w", w=66)[:, :, 0:HA],
                                    reads=[cx_buf[sub][1]], writes=[c_gatA])
                            b = bank()
                            transpose_group([(psum[b][0:3 * HA, 0:128], gatA[:, :], ident[:, :])],
                                            reads=[c_gatA, c_ident], wbuf=psum_buf[b])
                            copy_op("act", stage[1][0:3 * HA, c * 128:(c + 1) * 128], psum[b][0:3 * HA, 0:128],
                                    reads=[psum_buf[b]], writes=[stage_buf[1]])
                if cc > 0:
                    conv_b(cc - 1)
                sl = next_unit("AB")
                w3 = slot_half(sl, 0)
                tbs = {}
                for sub in range(2):
                    for t in range(2):
                        bB = bank()
                        hr = [h_buf[kb][t] for kb in range(NCH)]
                        mm_group(psum[bB][:, 0:NT], [(w3[:, kb, sub * 128:(sub + 1) * 128], r_(hT[:, kb, tcols(t)]))
                                                     for kb in range(NCH)], reads=[slot_buf[sl]] + hr, wbuf=psum_buf[bB])
                        i = tA()
                        copy_op("act", tmpA[i][:, 0:NT], psum[bB][:, 0:NT], reads=[psum_buf[bB]], writes=[tmpA_buf[i]])
                        tbs[(sub, t)] = i
                for sub in range(2):
                    c = 2 * cc + sub
                    ds = dsA[sub]
                    for t in range(2):
                        bA = bank()
                        nconv = layA[t][1] - HA
                        mm_group(psum[bA][:, 0:nconv], [(diagA[:, ds[k], :], r_(cx[sub][t][:, k:k + nconv]))
                                                        for k in range(HA + 1)],
                                 reads=[diagA_buf[d_] for d_ in ds] + [cx_buf[sub][t]], wbuf=psum_buf[bA])
                        i = tbs[(sub, t)]
                        for pi, (kind, row0, ln, tc0) in enumerate(pcs[t]):
                            es_ = layA[t][0][pi]
                            tt_op(r_(arena[:, c, tcols(t, tc0, tc0 + ln)]), psum[bA][:, es_ - HA:es_ - HA + ln],
                                  tmpA[i][:, tc0:tc0 + ln], ALU.mult, reads=[psum_buf[bA], tmpA_buf[i]],
                                  writes=[a_buf[c][t]])
                if cc == 3:
                    conv_b_pre(3)
                sl = next_unit("UU")
                w4 = slot4(sl)
                for sub in range(2):
                    c = 2 * cc + sub
                    for t in range(2):
                        b1, b2 = bank(), bank()
                        hr = [h_buf[kb][t] for kb in range(NCH)]
                        mm_group(psum[b1][:, 0:NT], [(w4[:, 0, kb, sub * 128:(sub + 1) * 128], r_(hT[:, kb, tcols(t)]))
                                                     for kb in range(NCH)], reads=[slot_buf[sl]] + hr, wbuf=psum_buf[b1])
                        mm_group(psum[b2][:, 0:NT], [(w4[:, 1, kb, sub * 128:(sub + 1) * 128], r_(hT[:, kb, tcols(t)]))
                                                     for kb in range(NCH)], reads=[slot_buf[sl]] + hr, wbuf=psum_buf[b2])
                        i = tA()
                        act_op(tmpA[i][:, 0:NT], psum[b2][:, 0:NT], AF.Sigmoid, reads=[psum_buf[b2]], writes=[tmpA_buf[i]])
                        for pi, (kind, row0, ln, tc0) in enumerate(pcs[t]):
                            es_ = layB[t][0][pi]
                            src, rb = halo_src(s, t, sub, c, kind, row0, "B")
                            copy_op("pool", r_(gx[sub][t][:, es_ - HB:es_]), src, reads=rb, writes=[gx_buf[sub][t]])
                            tt_op(r_(gx[sub][t][:, es_:es_ + ln]), psum[b1][:, tc0:tc0 + ln], tmpA[i][:, tc0:tc0 + ln],
                                  ALU.mult, reads=[psum_buf[b1], tmpA_buf[i]], writes=[gx_buf[sub][t]])
                        if t == 1 and s < NST - 1:
                            copy_op("pool", carryB[:, c, :], gx[sub][1][:, NT:NT + HB],
                                    reads=[gx_buf[sub][1]], writes=[c_carryB[c]])
                        if t == 1 and s == NST - 1:
                            copy_op("pool", gatB[:, :].rearrange("p (s w) -> p s w", w=HB),
                                    gx[sub][1][:, 144:144 + 3 * 94].rearrange("p (s w) -> p s w", w=94)[:, :, 0:HB],
                                    reads=[gx_buf[sub][1]], writes=[c_gatB])
                            b = bank()
                            transpose_group([(psum[b][0:3 * HB, 0:128], gatB[:, :], ident[:, :])],
                                            reads=[c_gatB, c_ident], wbuf=psum_buf[b])
                            copy_op("act", stage[0][0:3 * HB, c * 128:(c + 1) * 128], psum[b][0:3 * HB, 0:128],
                                    reads=[psum_buf[b]], writes=[stage_buf[0]])
            dummy_act(AF.Sqrt)
            conv_b(3)
            if s == NST - 1:
                out_toks.append(P.dma("sp", stage_st[0], ncb_p, stage[0][0:30, :], reads=[stage_buf[0]]))
                out_toks.append(P.dma("sp", stage_st[0], ncb_s, stage[0][30:90, :], reads=[stage_buf[0]]))
                out_toks.append(P.dma("sp", stage_st[1], nca_p, stage[1][0:2, :], reads=[stage_buf[1]]))
                out_toks.append(P.dma("sp", stage_st[1], nca_s, stage[1][2:6, :], reads=[stage_buf[1]]))

            lnb = []
            for t in range(2):
                bS1, bS2 = bank(), bank()
                for c in range(NCH):
                    i = tQ()
                    act_op(tmpQ[i][:, :], arena[:, 8 + c, tcols(t)], AF.Square,
                           reads=[a_buf[8 + c][t]], writes=[tmpQ_buf[i]])
                    mm_group(psum[bS1][:, 0:NT], [(ones[:, :], r_(arena[:, 8 + c, tcols(t)]))],
                             reads=[a_buf[8 + c][t], c_ones], wbuf=psum_buf[bS1], first=(c == 0), last=(c == NCH - 1))
                    mm_group(psum[bS2][:, 0:NT], [(ones[:, :], tmpQ[i][:, :])],
                             reads=[tmpQ_buf[i], c_ones], wbuf=psum_buf[bS2], first=(c == 0), last=(c == NCH - 1))
                lnb.append((bS1, bS2))
            lni = []
            for t in range(2):
                bS1, bS2 = lnb[t]
                im, iq, iv = tS(), tS(), tS()
                lni.append((im, iq, iv))
                ts_op(stt[im][:, :], psum[bS1][:, 0:NT], 1.0 / D, None, ALU.mult, ALU.bypass,
                      reads=[psum_buf[bS1]], writes=[stt_buf[im]])
                tt_op(stt[iq][:, :], stt[im][:, :], stt[im][:, :], ALU.mult, reads=[stt_buf[im]], writes=[stt_buf[iq]])
                stt_op(stt[iv][:, :], psum[bS2][:, 0:NT], 1.0 / D, stt[iq][:, :], ALU.mult, ALU.subtract,
                       reads=[psum_buf[bS2], stt_buf[iq]], writes=[stt_buf[iv]])
                ts_op(stt[iv][:, :], stt[iv][:, :], 0.0, None, ALU.max, ALU.bypass,
                      reads=[stt_buf[iv]], writes=[stt_buf[iv]])
            for t in range(2):
                im, iq, iv = lni[t]
                act_op(stt[iq][:, :], stt[iv][:, :], AF.Sqrt, reads=[stt_buf[iv], c_eps], writes=[stt_buf[iq]],
                       bias=epst[:, 0:1], scale=1.0)
            dummy_act(AF.Silu)

            def ln_centre(t):
                im, iq, iv = lni[t]
                for c in range(NCH):
                    tt_op(r_(arena[:, 8 + c, tcols(t)]), arena[:, 8 + c, tcols(t)], stt[im][:, :], ALU.subtract,
                          reads=[a_buf[8 + c][t], stt_buf[im]], writes=[a_buf[8 + c][t]], engine="dve")

            def ln_apply(t):
                im, iq, iv = lni[t]

                def fn(eng, iq=iq, iv=iv):
                    return eng.reciprocal(out=stt[iv][:, :], in_=stt[iq][:, :])
                P.op("dve", fn, reads=[stt_buf[iq]], writes=[stt_buf[iv]])
                for c in range(NCH):
                    j2 = tD()
                    stt_op(tmpD[j2][:, 0:NT], arena[:, 8 + c, tcols(t)], cvec[:, c, R_LNG:R_LNG + 1], stt[iv][:, :],
                           ALU.mult, ALU.mult, reads=[a_buf[8 + c][t], stt_buf[iv], c_cvec], writes=[tmpD_buf[j2]])
                    act_op(r_(arena[:, 8 + c, tcols(t)]), tmpD[j2][:, 0:NT], AF.Silu,
                           reads=[tmpD_buf[j2], c_cvec], writes=[a_buf[8 + c][t]], bias=cvec[:, c, R_LNB:R_LNB + 1])

            ln_centre(0)
            ln_apply(0)
            ln_centre(1)
            ln_pending = [lambda: ln_apply(1)]
            if not (KMAJOR and LN_SPLIT):
                ln_pending.pop()()

            for ff in range(4):
                slg = next_unit("GG")
                sla = next_unit("AO", hold=1)
                wg, wa = slot4(slg), slot4(sla)
                combos = [(sub, t) for sub in range(2) for t in range(2)]
                if ff == 0:
                    combos = [(sub, t) for t in range(2) for sub in range(2)]
                preM = {}

                def merge_pre(t):
                    bk = {}
                    hr = [h_buf[kb][t] for kb in range(NCH)]
                    for sub in range(2):
                        bGA, bGB, bYA = bank(), bank(), bank()
                        bk[("GA", sub)], bk[("GB", sub)], bk[("YA", sub)] = bGA, bGB, bYA
                        mm_group(psum[bGA][:, 0:NT], [(wg[:, 0, kb, sub * 128:(sub + 1) * 128], r_(hT[:, kb, tcols(t)]))
                                                      for kb in range(NCH)], reads=[slot_buf[slg]] + hr, wbuf=psum_buf[bGA])
                        mm_group(psum[bGB][:, 0:NT], [(wg[:, 1, kb, sub * 128:(sub + 1) * 128], r_(hT[:, kb, tcols(t)]))
                                                      for kb in range(NCH)], reads=[slot_buf[slg]] + hr, wbuf=psum_buf[bGB])
                        mm_group(psum[bYA][:, 0:NT], [(wa[:, 0, kb, sub * 128:(sub + 1) * 128], r_(arena[:, kb, tcols(t)]))
                                                      for kb in range(NCH)],
                                 reads=[slot_buf[sla]] + [a_buf[kb][t] for kb in range(NCH)], wbuf=psum_buf[bYA])
                    bk[("YB", 0)], bk[("YB", 1)] = bank(), bank()
                    for kb in range(NCH):
                        def fn(eng, kb=kb, bk=bk, t=t, wa=wa):
                            ins = None
                            for sub in range(2):
                                ins = eng.matmul(psum[bk[("YB", sub)]][:, 0:NT], lhsT=wa[:, 1, kb, sub * 128:(sub + 1) * 128],
                                                 rhs=r_(arena[:, 8 + kb, tcols(t)]), start=(kb == 0), stop=(kb == NCH - 1))
                            return ins
                        P.op("pe", fn, reads=[slot_buf[sla], a_buf[8 + kb][t]],
                             writes=[psum_buf[bk[("YB", 0)]], psum_buf[bk[("YB", 1)]]])
                    return bk

                for (sub, t) in combos:
                    f = 2 * ff + sub
                    if ff == 0 and KMAJOR:
                        if t not in preM:
                            if t == 1 and ln_pending:
                                ln_pending.pop()()
                            preM[t] = merge_pre(t)
                        bYA, bYB, bGA, bGB = (preM[t][("YA", sub)], preM[t][("YB", sub)],
                                              preM[t][("GA", sub)], preM[t][("GB", sub)])
                    else:
                        bYA, bYB, bGA, bGB = bank(), bank(), bank(), bank()
                        hr = [h_buf[kb][t] for kb in range(NCH)]
                        mm_group(psum[bGA][:, 0:NT], [(wg[:, 0, kb, sub * 128:(sub + 1) * 128], r_(hT[:, kb, tcols(t)]))
                                                      for kb in range(NCH)], reads=[slot_buf[slg]] + hr, wbuf=psum_buf[bGA])
                        mm_group(psum[bGB][:, 0:NT], [(wg[:, 1, kb, sub * 128:(sub + 1) * 128], r_(hT[:, kb, tcols(t)]))
                                                      for kb in range(NCH)], reads=[slot_buf[slg]] + hr, wbuf=psum_buf[bGB])
                        mm_group(psum[bYA][:, 0:NT], [(wa[:, 0, kb, sub * 128:(sub + 1) * 128], r_(arena[:, kb, tcols(t)]))
                                                      for kb in range(NCH)],
                                 reads=[slot_buf[sla]] + [a_buf[kb][t] for kb in range(NCH)], wbuf=psum_buf[bYA])
                        mm_group(psum[bYB][:, 0:NT], [(wa[:, 1, kb, sub * 128:(sub + 1) * 128], r_(arena[:, 8 + kb, tcols(t)]))
                                                      for kb in range(NCH)],
                                 reads=[slot_buf[sla]] + [a_buf[8 + kb][t] for kb in range(NCH)], wbuf=psum_buf[bYB])
                    if True:
                        i1, i2 = tA(), tA()
                        act_op(tmpA[i1][:, 0:NT], psum[bGA][:, 0:NT], AF.Sigmoid, reads=[psum_buf[bGA]], writes=[tmpA_buf[i1]])
                        act_op(tmpA[i2][:, 0:NT], psum[bGB][:, 0:NT], AF.Sigmoid, reads=[psum_buf[bGB]], writes=[tmpA_buf[i2]])
                        j1, j2 = tD(), tD()
                        tt_op(tmpD[j1][:, 0:NT], psum[bYA][:, 0:NT], tmpA[i1][:, 0:NT], ALU.mult,
                              reads=[psum_buf[bYA], tmpA_buf[i1]], writes=[tmpD_buf[j1]])
                        tt_op(tmpD[j2][:, 0:NT], psum[bYB][:, 0:NT], tmpA[i2][:, 0:NT], ALU.mult,
                              reads=[psum_buf[bYB], tmpA_buf[i2]], writes=[tmpD_buf[j2]])
                        tt_op(r_(arena[:, 16 + f, tcols(t)]), tmpD[j1][:, 0:NT], tmpD[j2][:, 0:NT], ALU.add,
                              reads=[tmpD_buf[j1], tmpD_buf[j2]], writes=[a_buf[16 + f][t]])
            dummy_act(AF.Sqrt)
            for fp in range(2):
                sl = next_unit("WO")
                w4 = slot4(sl)
                if fp == 0 and KMAJOR:
                    def wo_first(w4, sl, t):
                        bk = {(g, sub): bank() for g in range(2) for sub in range(2)}
                        for kb in range(NCH):
                            def fn(eng, kb=kb, bk=bk, t=t):
                                ins = None
                                for g in range(2):
                                    for sub in range(2):
                                        ins = eng.matmul(psum[bk[(g, sub)]][:, 0:NT],
                                                         lhsT=w4[:, g, kb, sub * 128:(sub + 1) * 128],
                                                         rhs=r_(arena[:, 16 + kb, tcols(t)]),
                                                         start=(kb == 0), stop=(kb == NCH - 1))
                                return ins
                            P.op("pe", fn, reads=[slot_buf[sl], a_buf[16 + kb][t]],
                                 writes=[psum_buf[b_] for b_ in bk.values()])
                        for g in range(2):
                            for sub in range(2):
                                f = 2 * g + sub
                                bO = bk[(g, sub)]
                                tt_op(xT[:, f, tcols(t)], psum[bO][:, 0:NT], xT[:, f, tcols(t)], ALU.add,
                                      reads=[psum_buf[bO], x_buf[f][t]], writes=[x_buf[f][t]])
                    wo_first(w4, sl, 0)
                    wo_first(w4, sl, 1)
                    continue
                for g in range(2):
                    for sub in range(2):
                        f = 4 * fp + 2 * g + sub
                        for t in range(2):
                            bO = bank()
                            mm_group(psum[bO][:, 0:NT],
                                     [(w4[:, g, kb, sub * 128:(sub + 1) * 128], r_(arena[:, 16 + kb, tcols(t)]))
                                      for kb in range(NCH)],
                                     reads=[slot_buf[sl]] + [a_buf[16 + kb][t] for kb in range(NCH)], wbuf=psum_buf[bO])
                            tt_op(xT[:, f, tcols(t)], psum[bO][:, 0:NT], xT[:, f, tcols(t)], ALU.add,
                                  reads=[psum_buf[bO], x_buf[f][t]], writes=[x_buf[f][t]])

        def store_y(s):
            for (kind, r0, nb, c0) in io_blocks(s):
                sg = rr("stage", 2)
                for g in range(2):
                    b = bank()
                    rb = [x_buf[4 * g + i][t] for i in range(4) for t in tiles_of(c0, nb)]
                    transpose_group([(psum[b][0:nb, i * 128:(i + 1) * 128], xT[:, 4 * g + i, c0:c0 + nb], ident[:, :])
                                     for i in range(4)], reads=rb + [c_ident], wbuf=psum_buf[b])
                    copy_op("act" if g == 0 else "dve", stage[sg][0:nb, g * 512:(g + 1) * 512], psum[b][0:nb, :],
                            reads=[psum_buf[b]], writes=[stage_buf[sg]])
                if kind == "P":
                    dst = y_p[r0:r0 + nb, :]
                else:
                    o = 0 if kind == "S0" else 64
                    dst = y_s[o + r0:o + r0 + nb, :]
                out_toks.append(P.dma("sp", stage_st[sg], dst, stage[sg][0:nb, :], reads=[stage_buf[sg]]))

        def dump():
            allb = [b_ for l in x_buf for b_ in l] + [b_ for l in h_buf for b_ in l] + [b_ for l in a_buf for b_ in l]
            out_toks.append(P.dma("sp", misc_sem, dbg_x, xT[:, :, :].rearrange("p c t -> p (c t)"), reads=allb))
            out_toks.append(P.dma("sp", misc_sem, dbg_h, hT[:, :, :].rearrange("p c t -> p (c t)"), reads=allb))
            out_toks.append(P.dma("sp", misc_sem, dbg_a, arena[:, :, :].rearrange("p c t -> p (c t)"), reads=allb))

        phases = []
        for s in range(NST):
            phases += [("load", lambda s=s: load_x(s, prefetched=((0, 1, 2) if (s > 0 and dbg is None) else ()))),
                       ("norm1", lambda: rms_norm(R_FFN1, False, AF.Silu)),
                       ("ffn1", lambda: ffn(0)), ("norm2", lambda: rms_norm(R_MIX, False, AF.Sigmoid)),
                       ("mixer", lambda s=s: mixer(s)), ("norm3", lambda: rms_norm(R_FFN2, False, AF.Silu)),
                       ("ffn2", lambda s=s: ffn(1, after_gu=((lambda: prefetch_x(s + 1))
                                                             if (s + 1 < NST and dbg is None) else None))),
                       ("norm4", lambda s=s: (rms_norm(R_FIN, True),
                                              prefetch_x(s + 1, ks=(2,)) if (s + 1 < NST and dbg is None) else None)),
                       ("store", lambda s=s: store_y(s))]
        for pi, (name, fn_) in enumerate(phases):
            fn_()
            if dbg is not None and dbg == (pi // 9, name):
                dump()
                break
        if dbg is None:
            assert ustate["next"] == len(units)
        P.wait_all("sp", out_toks)

        with nc.Block() as block:
            @block.sync
            def _(eng):
                for f in P.streams["sp"]:
                    f(eng)

            @block.gpsimd
            def _(eng):
                for f in P.streams["pool"]:
                    f(eng)

            @block.tensor
            def _(eng):
                for f in P.streams["pe"]:
                    f(eng)

            @block.vector
            def _(eng):
                for f in P.streams["dve"]:
                    f(eng)

            @block.scalar
            def _(eng):
                for f in P.streams["act"]:
                    f(eng)
    return nc


_NC_CACHE = {}


def kernel(x_prompt, x_sample, cache_conv_a, cache_conv_b, ffn1_norm, ffn1_w_gate_up, ffn1_w_down, mix_norm,
           w_in, conv_a_w, conv_b_w, conv_b_bias, conv_b_ln_g, conv_b_ln_b, w_a_out, w_b_out, w_out,
           ffn2_norm, ffn2_w_gate_up, ffn2_w_down, final_norm):
    f32 = np.float32
    A = lambda a: np.ascontiguousarray(np.asarray(a, dtype=f32))
    n = 8
    if "nc" not in _NC_CACHE:
        _NC_CACHE["nc"] = build_nc()
    nc = _NC_CACHE["nc"]
    cst = np.concatenate([A(ffn1_norm).reshape(1, D), A(mix_norm).reshape(1, D), A(ffn2_norm).reshape(1, D),
                          A(final_norm).reshape(1, D), A(conv_b_bias).reshape(1, D), A(conv_b_ln_g).reshape(1, D),
                          A(conv_b_ln_b).reshape(1, D), A(conv_a_w).reshape(3, D), A(conv_b_w).reshape(31, D)], axis=0)
    shared = {
        "cst": np.ascontiguousarray(cst),
        "ident_in": np.eye(128, dtype=f32),
        "w_gu1": A(ffn1_w_gate_up)[0], "w_gu2": A(ffn2_w_gate_up)[0],
        "w_dn1": A(ffn1_w_down)[0], "w_dn2": A(ffn2_w_down)[0],
        "w_in": A(w_in)[0], "w_ao": A(w_a_out)[0], "w_bo": A(w_b_out)[0], "w_o": A(w_out)[0],
    }
    xp, xs = A(x_prompt), A(x_sample)
    ca, cb = A(cache_conv_a)[0], A(cache_conv_b)[0]
    in_maps = []
    for i in range(n):
        m = dict(shared)
        m["x_p"] = xp[i]
        m["x_s"] = np.ascontiguousarray(xs[2 * i:2 * i + 2].reshape(128, D))
        m["hist"] = np.ascontiguousarray(np.concatenate([cb[2 * i], cb[2 * i + 1], ca[2 * i], ca[2 * i + 1]], axis=0))
        in_maps.append(m)
    res = run_bass_kernel_spmd(nc, in_maps, core_ids=list(range(n)))
    R = res.results
    y_prompt = np.stack([R[i]["y_p"] for i in range(n)], axis=0).astype(f32)
    y_sample = np.concatenate([R[i]["y_s"].reshape(2, 64, D) for i in range(n)], axis=0).astype(f32)
    nca_p = np.stack([R[i]["nca_p"] for i in range(n)], axis=0)[None].astype(f32)
    ncb_p = np.stack([R[i]["ncb_p"] for i in range(n)], axis=0)[None].astype(f32)
    nca_s = np.concatenate([R[i]["nca_s"].reshape(2, 2, D) for i in range(n)], axis=0)[None].astype(f32)
    ncb_s = np.concatenate([R[i]["ncb_s"].reshape(2, 30, D) for i in range(n)], axis=0)[None].astype(f32)
    return (y_prompt, y_sample, nca_p, ncb_p, nca_s, ncb_s)
```

```python
import numpy as np
import concourse.bass as bass
import concourse.mybir as mybir
from concourse.bass_utils import run_bass_kernel_spmd

F32 = mybir.dt.float32
F32R = mybir.dt.float32r
ALU = mybir.AluOpType
AF = mybir.ActivationFunctionType

D = 1024
DFF = 2816
NCH = 8
NJ = 22
TS = 544
NT = 272
NST = 4
HB = 30
HA = 2
EPS = 1e-6
GXW = 432
CXW = 344
NDIAG = 16
NSLOT = 4
XPOSE_MM = True
KMAJOR = True
LN_SPLIT = True
DIAG_ENG = "dve"
CG = 8
R_FFN1, R_MIX, R_FFN2, R_FIN, R_CBB, R_LNG, R_LNB, R_CAW, R_CBW = 0, 1, 2, 3, 4, 5, 6, 7, 10
NCR = 41


class Buf:
    __slots__ = ("lw", "rd")

    def __init__(self):
        self.lw = None
        self.rd = []


class Prog:
    ENG = ("pe", "act", "dve", "pool", "sp")

    def __init__(self):
        self.streams = {e: [] for e in self.ENG}
        self.cnt = {e: 0 for e in self.ENG}
        self.waited = {e: {} for e in self.ENG}
        self.dcnt = {}
        self.sems = {}

    def _deps(self, engine, reads, writes):
        need = {}
        for b in reads:
            if b.lw is not None:
                k, v = b.lw
                need[k] = max(need.get(k, 0), v)
        for b in writes:
            if b.lw is not None:
                k, v = b.lw
                need[k] = max(need.get(k, 0), v)
            for (k, v) in b.rd:
                need[k] = max(need.get(k, 0), v)
        w = self.waited[engine]
        waits = []
        if engine == "pe":
            need.pop("pe", None)
        for k, v in need.items():
            if w.get(k, 0) < v:
                waits.append((k, v))
                w[k] = v
        return waits

    def _mark(self, tok, reads, writes):
        for b in reads:
            b.rd.append(tok)
        for b in writes:
            b.lw = tok
            b.rd = []

    def op(self, engine, fn, reads=(), writes=()):
        waits = self._deps(engine, reads, writes)
        self.cnt[engine] += 1
        tok = (engine, self.cnt[engine])
        sems = self.sems

        def emit(eng):
            for (k, v) in waits:
                eng.wait_ge(sems[k], v)
            ins = fn(eng)
            ins.then_inc(sems[engine], 1)
        self.streams[engine].append(emit)
        self._mark(tok, reads, writes)
        return tok

    def dma(self, queue, dsem, out_ap, in_ap, reads=(), writes=(), cont=False):
        waits = self._deps(queue, reads, writes)
        if cont:
            waits = [(k, v) for (k, v) in waits if k != dsem]
        self.dcnt[dsem] = self.dcnt.get(dsem, 0) + 16
        tok = (dsem, self.dcnt[dsem])
        sems = self.sems

        def emit(eng):
            for (k, v) in waits:
                eng.wait_ge(sems[k], v)
            eng.dma_start(out=out_ap, in_=in_ap).then_inc(sems[dsem], 16)
        self.streams[queue].append(emit)
        self._mark(tok, reads, writes)
        return tok

    def wait_all(self, engine, toks):
        need = {}
        for (k, v) in toks:
            need[k] = max(need.get(k, 0), v)
        sems = self.sems
        lst = list(need.items())

        def emit(eng):
            for (k, v) in lst:
                eng.wait_ge(sems[k], v)
        self.streams[engine].append(emit)


def st_segments(s):
    if s < 3:
        return [("P", TS * s, TS, 0)]
    return [("P", 1632, 416, 0), ("S0", 0, 64, 416), ("S1", 0, 64, 480)]


def tile_pieces(s, t):
    out = []
    lo, hi = NT * t, NT * (t + 1)
    for (kind, r0, ln, c0) in st_segments(s):
        a, b = max(lo, c0), min(hi, c0 + ln)
        if a < b:
            out.append((kind, r0 + (a - c0), b - a, a - lo))
    return out


def ext_layout(pieces, H):
    starts = []
    pos = 0
    for (_, _, ln, _) in pieces:
        starts.append(pos + H)
        pos += H + ln
    return starts, pos


def io_blocks(s):
    out = []
    for (kind, r0, ln, c0) in st_segments(s):
        o = 0
        while o < ln:
            nb = min(128, ln - o)
            out.append((kind, r0 + o, nb, c0 + o))
            o += nb
    return out


def build_nc(dbg=None):
    nc = bass.Bass("TRN2", target_bir_lowering=False)

    def din(name, shape):
        return nc.dram_tensor(name, shape, F32, kind="ExternalInput").ap()

    def dout(name, shape):
        return nc.dram_tensor(name, shape, F32, kind="ExternalOutput").ap()

    x_p = din("x_p", [2048, D])
    x_s = din("x_s", [128, D])
    hist_in = din("hist", [64, D])
    cst_in = din("cst", [NCR, D])
    ident_in = din("ident_in", [128, 128])
    w_gu = [din("w_gu1", [D, 2 * DFF]), din("w_gu2", [D, 2 * DFF])]
    w_dn = [din("w_dn1", [DFF, D]), din("w_dn2", [DFF, D])]
    w_in = din("w_in", [D, 7 * D])
    w_ao = din("w_ao", [D, D])
    w_bo = din("w_bo", [D, D])
    w_o = din("w_o", [D, D])
    y_p = dout("y_p", [2048, D])
    y_s = dout("y_s", [128, D])
    nca_p = dout("nca_p", [2, D])
    ncb_p = dout("ncb_p", [30, D])
    nca_s = dout("nca_s", [4, D])
    ncb_s = dout("ncb_s", [60, D])
    if dbg is not None:
        dbg_x = dout("dbg_x", [128, NCH * TS])
        dbg_h = dout("dbg_h", [128, NCH * TS])
        dbg_a = dout("dbg_a", [128, 24 * TS])

    def wv(w):
        return w.rearrange("(kb p) n -> p kb n", p=128)

    w_gu_v = [wv(w) for w in w_gu]
    w_dn_v = [wv(w) for w in w_dn]
    w_in_v = wv(w_in)
    w_ao_v, w_bo_v, w_o_v = wv(w_ao), wv(w_bo), wv(w_o)

    P = Prog()
    import contextlib
    with contextlib.ExitStack() as es:
        def sb(name, shape, dt=F32):
            return es.enter_context(nc.sbuf_tensor(name, shape, dt))

        def sem(name):
            s_ = es.enter_context(nc.semaphore(name))
            P.sems[name] = s_
            return name

        for e in Prog.ENG:
            sem(e)

        xT = sb("xT", [128, NCH, TS])
        hT = sb("hT", [128, NCH, TS])
        arena = sb("arena", [128, 24, TS])
        slots = [sb("wslot%d" % i, [128, 4096], F32R) for i in range(NSLOT)]
        slot_sem = [sem("wsl%d" % i) for i in range(NSLOT)]
        slot_buf = [Buf() for _ in range(NSLOT)]
        diag = sb("diag", [128, NDIAG, 128], F32R)
        diag_buf = [Buf() for _ in range(NDIAG)]
        diagA = sb("diagA", [128, 6, 128], F32R)
        diagA_buf = [Buf() for _ in range(6)]
        gx = [[sb("gx%d%d" % (a, b), [128, GXW]) for b in range(2)] for a in range(2)]
        gx_buf = [[Buf() for _ in range(2)] for _ in range(2)]
        cx = [[sb("cx%d%d" % (a, b), [128, CXW]) for b in range(2)] for a in range(2)]
        cx_buf = [[Buf() for _ in range(2)] for _ in range(2)]
        NTA, NTD, NSTT = 6, 4, 6
        tmpA_all = sb("tmpA_all", [128, NTA, NT])
        tmpA = [tmpA_all[:, i, :] for i in range(NTA)]
        tmpA_buf = [Buf() for _ in range(NTA)]
        tmpD_all = sb("tmpD_all", [128, NTD, NT])
        tmpD = [tmpD_all[:, i, :] for i in range(NTD)]
        tmpD_buf = [Buf() for _ in range(NTD)]
        NTQ = 3
        tmpQ = [sb("tmpQ%d" % i, [128, NT], F32R) for i in range(NTQ)]
        tmpQ_buf = [Buf() for _ in range(NTQ)]
        stt_all = sb("stt_all", [128, NSTT, NT])
        stt = [stt_all[:, i, :] for i in range(NSTT)]
        stt_buf = [Buf() for _ in range(NSTT)]
        stage = [sb("stage%d" % i, [128, D]) for i in range(2)]
        stage_buf = [Buf() for _ in range(2)]
        stage_ld = [sem("stld%d" % i) for i in range(2)]
        LD = [tmpA_all[:, 0:4, :].rearrange("p a n -> p (a n)"), tmpD_all[:, 0:4, :].rearrange("p a n -> p (a n)"),
              stt_all[:, 0:4, :].rearrange("p a n -> p (a n)")]
        LD_buf = [tmpA_buf[0:4], tmpD_buf[0:4], stt_buf[0:4]]
        ld_sem = [sem("xld%d" % i) for i in range(3)]
        ld_sem_sw = [sem("xlds%d" % i) for i in range(3)]
        stage_st = [sem("stst%d" % i) for i in range(2)]
        ident = sb("ident", [128, 128])
        ones = sb("ones", [128, 128], F32R)
        ones_f = sb("ones_f", [128, 128])
        c_onesf = Buf()
        epst = sb("epst", [128, 1])
        dmy = sb("dmy", [128, 1])
        c_dmy = Buf()
        cvec = sb("cvec", [128, NCH, NCR])
        hist = sb("hist_sb", [128, NCH, 64])
        carryB = sb("carryB", [128, NCH, HB])
        carryA = sb("carryA", [128, NCH, HA])
        gatB = sb("gatB", [128, 3 * HB])
        gatA = sb("gatA", [128, 3 * HA])
        c_ident, c_ones, c_eps, c_cvec, c_hist = Buf(), Buf(), Buf(), Buf(), Buf()
        c_carryB = [Buf() for _ in range(NCH)]
        c_carryA = [Buf() for _ in range(NCH)]
        c_gatB, c_gatA = Buf(), Buf()
        psum = [es.enter_context(nc.psum_tensor("ps%d" % i, [128, 512], F32)) for i in range(8)]
        psum_buf = [Buf() for _ in range(8)]
        x_buf = [[Buf() for _ in range(2)] for _ in range(NCH)]
        h_buf = [[Buf() for _ in range(2)] for _ in range(NCH)]
        a_buf = [[Buf() for _ in range(2)] for _ in range(24)]
        misc_sem = sem("misc")
        out_toks = []

        state = {"bank": 0, "diagA": 0, "tq": 0, "ta": 0, "td": 0, "st": 0, "slot": 0, "diag": 0, "stage": 0}

        def rr(key, n):
            v = state[key]
            state[key] = (v + 1) % n
            return v

        def bank():
            return rr("bank", 8)

        def tA():
            return rr("ta", NTA)

        def tD():
            return rr("td", NTD)

        def tQ():
            return rr("tq", NTQ)

        def tS():
            return rr("st", NSTT)

        def r_(ap):
            return ap.bitcast(F32R)

        def tcols(t, a=0, b=NT):
            return slice(NT * t + a, NT * t + b)

        def mm_group(out_ap, pairs, reads, wbuf, first=True, last=True):
            n = len(pairs)

            def fn(eng):
                ins = None
                for i, (l, r) in enumerate(pairs):
                    ins = eng.matmul(out_ap, lhsT=l, rhs=r, start=(first and i == 0),
                                     stop=(last and i == n - 1))
                return ins
            return P.op("pe", fn, reads=reads, writes=[wbuf])

        def act_op(out_ap, in_ap, func, reads, writes, bias=None, scale=None):
            def fn(eng):
                kw = {}
                if bias is not None:
                    kw["bias"] = bias
                if scale is not None:
                    kw["scale"] = scale
                return eng.activation(out=out_ap, in_=in_ap, func=func, **kw)
            return P.op("act", fn, reads=reads, writes=writes)

        def dummy_act(func):
            return act_op(dmy[:, 0:1], epst[:, 0:1], func, reads=[c_eps], writes=[c_dmy])

        def tt_op(out_ap, in0, in1, op, reads, writes, engine="dve"):
            def fn(eng):
                return eng.tensor_tensor(out=out_ap, in0=in0, in1=in1, op=op)
            return P.op(engine, fn, reads=reads, writes=writes)

        def stt_op(out_ap, in0, scalar, in1, op0, op1, reads, writes):
            def fn(eng):
                return eng.scalar_tensor_tensor(out=out_ap, in0=in0, scalar=scalar, in1=in1, op0=op0, op1=op1)
            return P.op("dve", fn, reads=reads, writes=writes)

        def ts_op(out_ap, in0, s1, s2, op0, op1, reads, writes, engine="dve"):
            def fn(eng):
                return eng.tensor_scalar(out=out_ap, in0=in0, scalar1=s1, scalar2=s2, op0=op0, op1=op1)
            return P.op(engine, fn, reads=reads, writes=writes)

        def copy_op(engine, out_ap, in_ap, reads, writes):
            def fn(eng):
                if engine == "act":
                    return eng.copy(out=out_ap, in_=in_ap)
                return eng.tensor_copy(out=out_ap, in_=in_ap)
            return P.op(engine, fn, reads=reads, writes=writes)

        def transpose_group(items, reads, wbuf):
            def fn(eng):
                ins = None
                for (o, i_, idn) in items:
                    if XPOSE_MM:
                        ins = eng.matmul(o, lhsT=i_, rhs=idn, start=True, stop=True)
                    else:
                        ins = eng.transpose(out=o, in_=i_, identity=idn)
                return ins
            return P.op("pe", fn, reads=reads, writes=[wbuf])

        units = []
        for s in range(NST):
            for which in range(2):
                if which == 1:
                    for cc in range(4):
                        units.append(("CV", [(w_in_v, 0, 8, 1024 + cc * 256), (w_in_v, 0, 8, 2048 + cc * 256)]))
                        units.append(("AB", [(w_in_v, 0, 8, cc * 256)]))
                        units.append(("UU", [(w_in_v, 0, 8, 3072 + cc * 256), (w_in_v, 0, 8, 4096 + cc * 256)]))
                    for ff in range(4):
                        units.append(("GG", [(w_in_v, 0, 8, 5120 + ff * 256), (w_in_v, 0, 8, 6144 + ff * 256)]))
                        units.append(("AO", [(w_ao_v, 0, 8, ff * 256), (w_bo_v, 0, 8, ff * 256)]))
                    for fp in range(2):
                        units.append(("WO", [(w_o_v, 0, 8, fp * 512), (w_o_v, 0, 8, fp * 512 + 256)]))
                for jj in range(11):
                    units.append(("GU", [(w_gu_v[which], 0, 8, jj * 256), (w_gu_v[which], 0, 8, DFF + jj * 256)]))
                for ff in range(4):
                    for half in range(2):
                        units.append(("DN", [(w_dn_v[which], 11 * half, 11, ff * 256)]))
        ustate = {"next": 0, "issued": 0}

        def issue_load(u):
            tag, halves = units[u]
            sl = u % NSLOT
            for hi, (wview, kb0, nkb, col0) in enumerate(halves):
                off = hi * 2048
                o = slots[sl][:, off:off + nkb * 256].rearrange("p (k n) -> p k n", n=256)
                i_ = wview[:, kb0:kb0 + nkb, col0:col0 + 256]
                P.dma("pool", slot_sem[sl], o, i_, writes=[slot_buf[sl]], cont=(hi > 0))

        def next_unit(tag, hold=0):
            u = ustate["next"]
            assert units[u][0] == tag, (units[u][0], tag)
            while ustate["issued"] < min(len(units), u + NSLOT - hold):
                issue_load(ustate["issued"])
                ustate["issued"] += 1
            ustate["next"] = u + 1
            return u % NSLOT

        def slot4(sl):
            return slots[sl][:, :].rearrange("p (h k n) -> p h k n", h=2, k=8, n=256)

        def slot_half(sl, hi):
            return slots[sl][:, hi * 2048:(hi + 1) * 2048].rearrange("p (k n) -> p k n", n=256)

        P.dma("sp", misc_sem, ident[:, :], ident_in, writes=[c_ident])
        P.op("pool", lambda eng: eng.memset(ones_f[:, :], 1.0), writes=[c_onesf])
        copy_op("pool", ones[:, :], ones_f[:, :], reads=[c_onesf], writes=[c_ones])
        P.op("pool", lambda eng: eng.memset(epst[:, :], EPS), writes=[c_eps])
        P.op("pool", lambda eng: eng.memset(carryB[:, :, :], 0.0), writes=c_carryB)
        P.op("pool", lambda eng: eng.memset(carryA[:, :, :], 0.0), writes=c_carryA)
        P.dma("sp", stage_ld[0], stage[0][0:NCR, :], cst_in, writes=[stage_buf[0]])
        P.dma("sp", stage_ld[1], stage[1][0:64, :], hist_in, writes=[stage_buf[1]])
        b = bank()
        transpose_group([(psum[b][:, c * 64:c * 64 + NCR], stage[0][0:NCR, c * 128:(c + 1) * 128], ident[0:NCR, 0:NCR])
                         for c in range(NCH)], reads=[stage_buf[0], c_ident], wbuf=psum_buf[b])
        copy_op("dve", cvec[:, :, :], psum[b][:, :].rearrange("p (c w) -> p c w", w=64)[:, :, 0:NCR],
                reads=[psum_buf[b]], writes=[c_cvec])
        b = bank()
        transpose_group([(psum[b][:, c * 64:(c + 1) * 64], stage[1][0:64, c * 128:(c + 1) * 128], ident[0:64, 0:64])
                         for c in range(NCH)], reads=[stage_buf[1], c_ident], wbuf=psum_buf[b])
        copy_op("act", hist[:, :, :], psum[b][:, :].rearrange("p (c w) -> p c w", w=64),
                reads=[psum_buf[b]], writes=[c_hist])

        def tiles_of(c0, nb):
            return [t for t in range(2) if c0 < NT * (t + 1) and c0 + nb > NT * t]

        def x_src(kind, r0, nb):
            if kind == "P":
                return x_p[r0:r0 + nb, :]
            o = 0 if kind == "S0" else 64
            return x_s[o + r0:o + r0 + nb, :]

        def prefetch_x(s, ks=(0, 1)):
            blocks = io_blocks(s)
            for k in ks:
                (kind, r0, nb, c0) = blocks[k]
                P.dma("sp", ld_sem[k % 3], LD[k % 3][0:nb, 0:D], x_src(kind, r0, nb), writes=LD_buf[k % 3])

        def load_x(s, prefetched=()):
            for k, (kind, r0, nb, c0) in enumerate(io_blocks(s)):
                sg = k % 3
                if k not in prefetched:
                    q = "pool" if prefetched else "sp"
                    P.dma(q, (ld_sem_sw if q == "pool" else ld_sem)[sg], LD[sg][0:nb, 0:D], x_src(kind, r0, nb),
                          writes=LD_buf[sg])
                for g in range(2):
                    b = bank()
                    transpose_group([(psum[b][:, i * 128:i * 128 + nb],
                                      LD[sg][0:nb, (4 * g + i) * 128:(4 * g + i + 1) * 128],
                                      ident[0:nb, 0:nb]) for i in range(4)],
                                    reads=LD_buf[sg] + [c_ident], wbuf=psum_buf[b])
                    wb = [x_buf[4 * g + i][t] for i in range(4) for t in tiles_of(c0, nb)]
                    copy_op("act" if g == 0 else "dve", xT[:, 4 * g:4 * g + 4, c0:c0 + nb],
                            psum[b][:, :].rearrange("p (i w) -> p i w", w=128)[:, :, 0:nb],
                            reads=[psum_buf[b]], writes=wb)

        def rms_norm(row, inplace, nxt=None):
            for t in range(2):
                b = bank()
                for f in range(NCH):
                    i = tQ()
                    if True:
                        act_op(tmpQ[i][:, :], xT[:, f, tcols(t)], AF.Square,
                               reads=[x_buf[f][t]], writes=[tmpQ_buf[i]])
                    else:
                        tt_op(tmpQ[i][:, :], xT[:, f, tcols(t)], xT[:, f, tcols(t)], ALU.mult,
                              reads=[x_buf[f][t]], writes=[tmpQ_buf[i]])
                    mm_group(psum[b][:, 0:NT], [(ones[:, :], tmpQ[i][:, :])],
                             reads=[tmpQ_buf[i], c_ones], wbuf=psum_buf[b], first=(f == 0), last=(f == NCH - 1))
                i1, i2 = tS(), tS()
                act_op(stt[i1][:, :], psum[b][:, 0:NT], AF.Sqrt, reads=[psum_buf[b], c_eps], writes=[stt_buf[i1]],
                       bias=epst[:, 0:1], scale=1.0 / D)
                if t == 1 and nxt is not None:
                    dummy_act(nxt)

                def fn(eng, i1=i1, i2=i2):
                    return eng.reciprocal(out=stt[i2][:, :], in_=stt[i1][:, :])
                P.op("dve", fn, reads=[stt_buf[i1]], writes=[stt_buf[i2]])
                for f in range(NCH):
                    if inplace:
                        o, wb = xT[:, f, tcols(t)], [x_buf[f][t]]
                    else:
                        o, wb = r_(hT[:, f, tcols(t)]), [h_buf[f][t]]
                    stt_op(o, xT[:, f, tcols(t)], cvec[:, f, row:row + 1], stt[i2][:, :], ALU.mult, ALU.mult,
                           reads=[x_buf[f][t], stt_buf[i2], c_cvec], writes=wb)

        def kmajor_first(w4, sl, t):
            bk = {(hf, sub): bank() for sub in range(2) for hf in range(2)}
            for kb in range(NCH):
                def fn(eng, kb=kb, bk=bk, t=t):
                    ins = None
                    for sub in range(2):
                        for hf in range(2):
                            ins = eng.matmul(psum[bk[(hf, sub)]][:, 0:NT], lhsT=w4[:, hf, kb, sub * 128:(sub + 1) * 128],
                                             rhs=r_(hT[:, kb, tcols(t)]), start=(kb == 0), stop=(kb == NCH - 1))
                    return ins
                P.op("pe", fn, reads=[slot_buf[sl], h_buf[kb][t]], writes=[psum_buf[b_] for b_ in bk.values()])
            return bk

        def ffn(which, after_gu=None):
            for jj in range(11):
                sl = next_unit("GU")
                w4 = slot4(sl)
                combos = [(sub, t) for sub in range(2) for t in range(2)]
                if jj == 0:
                    combos = [(sub, t) for t in range(2) for sub in range(2)]
                pre = {}
                for (sub, t) in combos:
                    j = 2 * jj + sub
                    if jj == 0 and KMAJOR:
                        if t not in pre:
                            pre[t] = kmajor_first(w4, sl, t)
                        bG, bU = pre[t][(0, sub)], pre[t][(1, sub)]
                    else:
                        bG, bU = bank(), bank()
                        hr = [h_buf[kb][t] for kb in range(NCH)]
                        mm_group(psum[bG][:, 0:NT], [(w4[:, 0, kb, sub * 128:(sub + 1) * 128], r_(hT[:, kb, tcols(t)]))
                                                     for kb in range(NCH)], reads=[slot_buf[sl]] + hr, wbuf=psum_buf[bG])
                        mm_group(psum[bU][:, 0:NT], [(w4[:, 1, kb, sub * 128:(sub + 1) * 128], r_(hT[:, kb, tcols(t)]))
                                                     for kb in range(NCH)], reads=[slot_buf[sl]] + hr, wbuf=psum_buf[bU])
                    if True:
                        i = tA()
                        act_op(tmpA[i][:, 0:NT], psum[bG][:, 0:NT], AF.Silu, reads=[psum_buf[bG]], writes=[tmpA_buf[i]])
                        tt_op(r_(arena[:, j, tcols(t)]), psum[bU][:, 0:NT], tmpA[i][:, 0:NT], ALU.mult,
                              reads=[psum_buf[bU], tmpA_buf[i]], writes=[a_buf[j][t]])
            dummy_act(AF.Sqrt)
            if after_gu is not None:
                after_gu()
            for ff in range(4):
                bks = [[bank() for t in range(2)] for sub in range(2)]
                for half in range(2):
                    sl = next_unit("DN")
                    wd = slots[sl][:, 0:11 * 256].rearrange("p (k n) -> p k n", n=256)
                    for sub in range(2):
                        for t in range(2):
                            bO = bks[sub][t]
                            mm_group(psum[bO][:, 0:NT],
                                     [(wd[:, kk, sub * 128:(sub + 1) * 128], r_(arena[:, 11 * half + kk, tcols(t)]))
                                      for kk in range(11)],
                                     reads=[slot_buf[sl]] + [a_buf[11 * half + kk][t] for kk in range(11)],
                                     wbuf=psum_buf[bO], first=(half == 0), last=(half == 1))
                for sub in range(2):
                    f = 2 * ff + sub
                    for t in range(2):
                        bO = bks[sub][t]
                        stt_op(xT[:, f, tcols(t)], psum[bO][:, 0:NT], 0.5, xT[:, f, tcols(t)], ALU.mult, ALU.add,
                               reads=[psum_buf[bO], x_buf[f][t]], writes=[x_buf[f][t]])

        def build_diag(c, row):
            d = rr("diag", NDIAG)
            ts_op(r_(diag[:, d, :]), ident[:, :], cvec[:, c, row:row + 1], 0.0, ALU.mult, ALU.add,
                  reads=[c_ident, c_cvec, ], writes=[diag_buf[d]], engine=DIAG_ENG)
            return d

        def build_diagA(c, row):
            d = rr("diagA", 6)
            ts_op(r_(diagA[:, d, :]), ident[:, :], cvec[:, c, row:row + 1], 0.0, ALU.mult, ALU.add,
                  reads=[c_ident, c_cvec, ], writes=[diagA_buf[d]], engine=DIAG_ENG)
            return d

        def halo_src(s, t, sub, c, kind, row0, which):
            H = HB if which == "B" else HA
            if kind == "P":
                if t == 0:
                    cr = carryB if which == "B" else carryA
                    cb_ = c_carryB if which == "B" else c_carryA
                    return cr[:, c, :], [cb_[c]]
                ext = gx if which == "B" else cx
                eb = gx_buf if which == "B" else cx_buf
                return ext[sub][0][:, NT:NT + H], [eb[sub][0]]
            if which == "B":
                o = 0 if kind == "S0" else 30
                return hist[:, c, o:o + 30], [c_hist]
            o = 60 if kind == "S0" else 62
            return hist[:, c, o:o + 2], [c_hist]

        def mixer(s):
            pcs = [tile_pieces(s, t) for t in range(2)]
            layA = [ext_layout(pcs[t], HA) for t in range(2)]
            layB = [ext_layout(pcs[t], HB) for t in range(2)]
            pend = {}

            def conv_b_pre(cc):
                pend[cc] = [build_diag(2 * cc, R_CBW + k) for k in range(CG)]

            def conv_b(cc):
                for sub in range(2):
                    c = 2 * cc + sub
                    bb = [bank(), bank()]
                    for k0 in range(0, HB + 1, CG):
                        ks = list(range(k0, min(HB + 1, k0 + CG)))
                        if sub == 0 and k0 == 0 and cc in pend:
                            ds = pend.pop(cc)
                        else:
                            ds = [build_diag(c, R_CBW + k) for k in ks]
                        for t in range(2):
                            nconv = layB[t][1] - HB
                            mm_group(psum[bb[t]][:, 0:nconv],
                                     [(diag[:, ds[i], :], r_(gx[sub][t][:, k:k + nconv])) for i, k in enumerate(ks)],
                                     reads=[diag_buf[d_] for d_ in ds] + [gx_buf[sub][t]], wbuf=psum_buf[bb[t]],
                                     first=(k0 == 0), last=(ks[-1] == HB))
                    for t in range(2):
                        for pi, (kind, row0, ln, tc0) in enumerate(pcs[t]):
                            es_ = layB[t][0][pi]
                            act_op(r_(arena[:, 8 + c, tcols(t, tc0, tc0 + ln)]), psum[bb[t]][:, es_ - HB:es_ - HB + ln],
                                   AF.Identity, reads=[psum_buf[bb[t]], c_cvec], writes=[a_buf[8 + c][t]],
                                   bias=cvec[:, c, R_CBB:R_CBB + 1])

            for cc in range(4):
                if cc > 0:
                    conv_b_pre(cc - 1)
                dsA = [[build_diagA(2 * cc + sub, R_CAW + k) for k in range(HA + 1)] for sub in range(2)]
                sl = next_unit("CV")
                w4 = slot4(sl)
                combos = [(sub, t) for sub in range(2) for t in range(2)]
                if cc == 0:
                    combos = [(sub, t) for t in range(2) for sub in range(2)]
                pre = {}
                for (sub, t) in combos:
                    c = 2 * cc + sub
                    if cc == 0 and KMAJOR:
                        if t not in pre:
                            pre[t] = kmajor_first(w4, sl, t)
                        bC, bV = pre[t][(0, sub)], pre[t][(1, sub)]
                    else:
                        bC, bV = bank(), bank()
                        hr = [h_buf[kb][t] for kb in range(NCH)]
                        mm_group(psum[bC][:, 0:NT], [(w4[:, 0, kb, sub * 128:(sub + 1) * 128], r_(hT[:, kb, tcols(t)]))
                                                     for kb in range(NCH)], reads=[slot_buf[sl]] + hr, wbuf=psum_buf[bC])
                        mm_group(psum[bV][:, 0:NT], [(w4[:, 1, kb, sub * 128:(sub + 1) * 128], r_(hT[:, kb, tcols(t)]))
                                                     for kb in range(NCH)], reads=[slot_buf[sl]] + hr, wbuf=psum_buf[bV])
                    if True:
                        i = tA()
                        copy_op("act", tmpA[i][:, 0:NT], psum[bC][:, 0:NT], reads=[psum_buf[bC]], writes=[tmpA_buf[i]])
                        for pi, (kind, row0, ln, tc0) in enumerate(pcs[t]):
                            es_ = layA[t][0][pi]
                            src, rb = halo_src(s, t, sub, c, kind, row0, "A")
                            copy_op("pool", r_(cx[sub][t][:, es_ - HA:es_]), src, reads=rb, writes=[cx_buf[sub][t]])
                            tt_op(r_(cx[sub][t][:, es_:es_ + ln]), psum[bV][:, tc0:tc0 + ln], tmpA[i][:, tc0:tc0 + ln],
                                  ALU.mult, reads=[psum_buf[bV], tmpA_buf[i]], writes=[cx_buf[sub][t]])
                        if t == 1 and s < NST - 1:
                            copy_op("pool", carryA[:, c, :], cx[sub][1][:, NT:NT + HA],
                                    reads=[cx_buf[sub][1]], writes=[c_carryA[c]])
                        if t == 1 and s == NST - 1:
                            copy_op("pool", gatA[:, :].rearrange("p (s w) -> p s w", w=HA),
                                    cx[sub][1][:, 144:144 + 3 * 66].rearrange("p (s w) -> p s w", w=66)[:, :, 0:HA],
                                    reads=[cx_buf[sub][1]], writes=[c_gatA])
                            b = bank()
                            transpose_group([(psum[b][0:3 * HA, 0:128], gatA[:, :], ident[:, :])],
                                            reads=[c_gatA, c_ident], wbuf=psum_buf[b])
                            copy_op("act", stage[1][0:3 * HA, c * 128:(c + 1) * 128], psum[b][0:3 * HA, 0:128],
                                    reads=[psum_buf[b]], writes=[stage_buf[1]])
                if cc > 0:
                    conv_b(cc - 1)
                sl = next_unit("AB")
                w3 = slot_half(sl, 0)
                tbs = {}
                for sub in range(2):
                    for t in range(2):
                        bB = bank()
                        hr = [h_buf[kb][t] for kb in range(NCH)]
                        mm_group(psum[bB][:, 0:NT], [(w3[:, kb, sub * 128:(sub + 1) * 128], r_(hT[:, kb, tcols(t)]))
                                                     for kb in range(NCH)], reads=[slot_buf[sl]] + hr, wbuf=psum_buf[bB])
                        i = tA()
                        copy_op("act", tmpA[i][:, 0:NT], psum[bB][:, 0:NT], reads=[psum_buf[bB]], writes=[tmpA_buf[i]])
                        tbs[(sub, t)] = i
                for sub in range(2):
                    c = 2 * cc + sub
                    ds = dsA[sub]
                    for t in range(2):
                        bA = bank()
                        nconv = layA[t][1] - HA
                        mm_group(psum[bA][:, 0:nconv], [(diagA[:, ds[k], :], r_(cx[sub][t][:, k:k + nconv]))
                                                        for k in range(HA + 1)],
                                 reads=[diagA_buf[d_] for d_ in ds] + [cx_buf[sub][t]], wbuf=psum_buf[bA])
                        i = tbs[(sub, t)]
                        for pi, (kind, row0, ln, tc0) in enumerate(pcs[t]):
                            es_ = layA[t][0][pi]
                            tt_op(r_(arena[:, c, tcols(t, tc0, tc0 + ln)]), psum[bA][:, es_ - HA:es_ - HA + ln],
                                  tmpA[i][:, tc0:tc0 + ln], ALU.mult, reads=[psum_buf[bA], tmpA_buf[i]],
                                  writes=[a_buf[c][t]])
                if cc == 3:
                    conv_b_pre(3)
                sl = next_unit("UU")
                w4 = slot4(sl)
                for sub in range(2):
                    c = 2 * cc + sub
                    for t in range(2):
                        b1, b2 = bank(), bank()
                        hr = [h_buf[kb][t] for kb in range(NCH)]
                        mm_group(psum[b1][:, 0:NT], [(w4[:, 0, kb, sub * 128:(sub + 1) * 128], r_(hT[:, kb, tcols(t)]))
                                                     for kb in range(NCH)], reads=[slot_buf[sl]] + hr, wbuf=psum_buf[b1])
                        mm_group(psum[b2][:, 0:NT], [(w4[:, 1, kb, sub * 128:(sub + 1) * 128], r_(hT[:, kb, tcols(t)]))
                                                     for kb in range(NCH)], reads=[slot_buf[sl]] + hr, wbuf=psum_buf[b2])
                        i = tA()
                        act_op(tmpA[i][:, 0:NT], psum[b2][:, 0:NT], AF.Sigmoid, reads=[psum_buf[b2]], writes=[tmpA_buf[i]])
                        for pi, (kind, row0, ln, tc0) in enumerate(pcs[t]):
                            es_ = layB[t][0][pi]
                            src, rb = halo_src(s, t, sub, c, kind, row0, "B")
                            copy_op("pool", r_(gx[sub][t][:, es_ - HB:es_]), src, reads=rb, writes=[gx_buf[sub][t]])
                            tt_op(r_(gx[sub][t][:, es_:es_ + ln]), psum[b1][:, tc0:tc0 + ln], tmpA[i][:, tc0:tc0 + ln],
                                  ALU.mult, reads=[psum_buf[b1], tmpA_buf[i]], writes=[gx_buf[sub][t]])
                        if t == 1 and s < NST - 1:
                            copy_op("pool", carryB[:, c, :], gx[sub][1][:, NT:NT + HB],
                                    reads=[gx_buf[sub][1]], writes=[c_carryB[c]])
                        if t == 1 and s == NST - 1:
                            copy_op("pool", gatB[:, :].rearrange("p (s w) -> p s w", w=HB),
                                    gx[sub][1][:, 144:144 + 3 * 94].rearrange("p (s w) -> p s w", w=94)[:, :, 0:HB],
                                    reads=[gx_buf[sub][1]], writes=[c_gatB])
                            b = bank()
                            transpose_group([(psum[b][0:3 * HB, 0:128], gatB[:, :], ident[:, :])],
                                            reads=[c_gatB, c_ident], wbuf=psum_buf[b])
                            copy_op("act", stage[0][0:3 * HB, c * 128:(c + 1) * 128], psum[b][0:3 * HB, 0:128],
                                    reads=[psum_buf[b]], writes=[stage_buf[0]])
            dummy_act(AF.Sqrt)
            conv_b(3)
            if s == NST - 1:
                out_toks.append(P.dma("sp", stage_st[0], ncb_p, stage[0][0:30, :], reads=[stage_buf[0]]))
                out_toks.append(P.dma("sp", stage_st[0], ncb_s, stage[0][30:90, :], reads=[stage_buf[0]]))
                out_toks.append(P.dma("sp", stage_st[1], nca_p, stage[1][0:2, :], reads=[stage_buf[1]]))
                out_toks.append(P.dma("sp", stage_st[1], nca_s, stage[1][2:6, :], reads=[stage_buf[1]]))

            lnb = []
            for t in range(2):
                bS1, bS2 = bank(), bank()
                for c in range(NCH):
                    i = tQ()
                    act_op(tmpQ[i][:, :], arena[:, 8 + c, tcols(t)], AF.Square,
                           reads=[a_buf[8 + c][t]], writes=[tmpQ_buf[i]])
                    mm_group(psum[bS1][:, 0:NT], [(ones[:, :], r_(arena[:, 8 + c, tcols(t)]))],
                             reads=[a_buf[8 + c][t], c_ones], wbuf=psum_buf[bS1], first=(c == 0), last=(c == NCH - 1))
                    mm_group(psum[bS2][:, 0:NT], [(ones[:, :], tmpQ[i][:, :])],
                             reads=[tmpQ_buf[i], c_ones], wbuf=psum_buf[bS2], first=(c == 0), last=(c == NCH - 1))
                lnb.append((bS1, bS2))
            lni = []
            for t in range(2):
                bS1, bS2 = lnb[t]
                im, iq, iv = tS(), tS(), tS()
                lni.append((im, iq, iv))
                ts_op(stt[im][:, :], psum[bS1][:, 0:NT], 1.0 / D, None, ALU.mult, ALU.bypass,
                      reads=[psum_buf[bS1]], writes=[stt_buf[im]])
                tt_op(stt[iq][:, :], stt[im][:, :], stt[im][:, :], ALU.mult, reads=[stt_buf[im]], writes=[stt_buf[iq]])
                stt_op(stt[iv][:, :], psum[bS2][:, 0:NT], 1.0 / D, stt[iq][:, :], ALU.mult, ALU.subtract,
                       reads=[psum_buf[bS2], stt_buf[iq]], writes=[stt_buf[iv]])
                ts_op(stt[iv][:, :], stt[iv][:, :], 0.0, None, ALU.max, ALU.bypass,
                      reads=[stt_buf[iv]], writes=[stt_buf[iv]])
            for t in range(2):
                im, iq, iv = lni[t]
                act_op(stt[iq][:, :], stt[iv][:, :], AF.Sqrt, reads=[stt_buf[iv], c_eps], writes=[stt_buf[iq]],
                       bias=epst[:, 0:1], scale=1.0)
            dummy_act(AF.Silu)

            def ln_centre(t):
                im, iq, iv = lni[t]
                for c in range(NCH):
                    tt_op(r_(arena[:, 8 + c, tcols(t)]), arena[:, 8 + c, tcols(t)], stt[im][:, :], ALU.subtract,
                          reads=[a_buf[8 + c][t], stt_buf[im]], writes=[a_buf[8 + c][t]], engine="dve")

            def ln_apply(t):
                im, iq, iv = lni[t]

                def fn(eng, iq=iq, iv=iv):
                    return eng.reciprocal(out=stt[iv][:, :], in_=stt[iq][:, :])
                P.op("dve", fn, reads=[stt_buf[iq]], writes=[stt_buf[iv]])
                for c in range(NCH):
                    j2 = tD()
                    stt_op(tmpD[j2][:, 0:NT], arena[:, 8 + c, tcols(t)], cvec[:, c, R_LNG:R_LNG + 1], stt[iv][:, :],
                           ALU.mult, ALU.mult, reads=[a_buf[8 + c][t], stt_buf[iv], c_cvec], writes=[tmpD_buf[j2]])
                    act_op(r_(arena[:, 8 + c, tcols(t)]), tmpD[j2][:, 0:NT], AF.Silu,
                           reads=[tmpD_buf[j2], c_cvec], writes=[a_buf[8 + c][t]], bias=cvec[:, c, R_LNB:R_LNB + 1])

            ln_centre(0)
            ln_apply(0)
            ln_centre(1)
            ln_pending = [lambda: ln_apply(1)]
            if not (KMAJOR and LN_SPLIT):
                ln_pending.pop()()

            for ff in range(4):
                slg = next_unit("GG")
                sla = next_unit("AO", hold=1)
                wg, wa = slot4(slg), slot4(sla)
                combos = [(sub, t) for sub in range(2) for t in range(2)]
                if ff == 0:
                    combos = [(sub, t) for t in range(2) for sub in range(2)]
                preM = {}

                def merge_pre(t):
                    bk = {}
                    hr = [h_buf[kb][t] for kb in range(NCH)]
                    for sub in range(2):
                        bGA, bGB, bYA = bank(), bank(), bank()
                        bk[("GA", sub)], bk[("GB", sub)], bk[("YA", sub)] = bGA, bGB, bYA
                        mm_group(psum[bGA][:, 0:NT], [(wg[:, 0, kb, sub * 128:(sub + 1) * 128], r_(hT[:, kb, tcols(t)]))
                                                      for kb in range(NCH)], reads=[slot_buf[slg]] + hr, wbuf=psum_buf[bGA])
                        mm_group(psum[bGB][:, 0:NT], [(wg[:, 1, kb, sub * 128:(sub + 1) * 128], r_(hT[:, kb, tcols(t)]))
                                                      for kb in range(NCH)], reads=[slot_buf[slg]] + hr, wbuf=psum_buf[bGB])
                        mm_group(psum[bYA][:, 0:NT], [(wa[:, 0, kb, sub * 128:(sub + 1) * 128], r_(arena[:, kb, tcols(t)]))
                                                      for kb in range(NCH)],
                                 reads=[slot_buf[sla]] + [a_buf[kb][t] for kb in range(NCH)], wbuf=psum_buf[bYA])
                    bk[("YB", 0)], bk[("YB", 1)] = bank(), bank()
                    for kb in range(NCH):
                        def fn(eng, kb=kb, bk=bk, t=t, wa=wa):
                            ins = None
                            for sub in range(2):
                                ins = eng.matmul(psum[bk[("YB", sub)]][:, 0:NT], lhsT=wa[:, 1, kb, sub * 128:(sub + 1) * 128],
                                                 rhs=r_(arena[:, 8 + kb, tcols(t)]), start=(kb == 0), stop=(kb == NCH - 1))
                            return ins
                        P.op("pe", fn, reads=[slot_buf[sla], a_buf[8 + kb][t]],
                             writes=[psum_buf[bk[("YB", 0)]], psum_buf[bk[("YB", 1)]]])
                    return bk

                for (sub, t) in combos:
                    f = 2 * ff + sub
                    if ff == 0 and KMAJOR:
                        if t not in preM:
                            if t == 1 and ln_pending:
                                dummy_act(AF.Silu)
                                ln_pending.pop()()
                                dummy_act(AF.Sigmoid)
                            preM[t] = merge_pre(t)
                        bYA, bYB, bGA, bGB = (preM[t][("YA", sub)], preM[t][("YB", sub)],
                                              preM[t][("GA", sub)], preM[t][("GB", sub)])
                    else:
                        bYA, bYB, bGA, bGB = bank(), bank(), bank(), bank()
                        hr = [h_buf[kb][t] for kb in range(NCH)]
                        mm_group(psum[bGA][:, 0:NT], [(wg[:, 0, kb, sub * 128:(sub + 1) * 128], r_(hT[:, kb, tcols(t)]))
                                                      for kb in range(NCH)], reads=[slot_buf[slg]] + hr, wbuf=psum_buf[bGA])
                        mm_group(psum[bGB][:, 0:NT], [(wg[:, 1, kb, sub * 128:(sub + 1) * 128], r_(hT[:, kb, tcols(t)]))
                                                      for kb in range(NCH)], reads=[slot_buf[slg]] + hr, wbuf=psum_buf[bGB])
                        mm_group(psum[bYA][:, 0:NT], [(wa[:, 0, kb, sub * 128:(sub + 1) * 128], r_(arena[:, kb, tcols(t)]))
                                                      for kb in range(NCH)],
                                 reads=[slot_buf[sla]] + [a_buf[kb][t] for kb in range(NCH)], wbuf=psum_buf[bYA])
                        mm_group(psum[bYB][:, 0:NT], [(wa[:, 1, kb, sub * 128:(sub + 1) * 128], r_(arena[:, 8 + kb, tcols(t)]))
                                                      for kb in range(NCH)],
                                 reads=[slot_buf[sla]] + [a_buf[8 + kb][t] for kb in range(NCH)], wbuf=psum_buf[bYB])
                    if True:
                        i1, i2 = tA(), tA()
                        act_op(tmpA[i1][:, 0:NT], psum[bGA][:, 0:NT], AF.Sigmoid, reads=[psum_buf[bGA]], writes=[tmpA_buf[i1]])
                        act_op(tmpA[i2][:, 0:NT], psum[bGB][:, 0:NT], AF.Sigmoid, reads=[psum_buf[bGB]], writes=[tmpA_buf[i2]])
                        j1, j2 = tD(), tD()
                        tt_op(tmpD[j1][:, 0:NT], psum[bYA][:, 0:NT], tmpA[i1][:, 0:NT], ALU.mult,
                              reads=[psum_buf[bYA], tmpA_buf[i1]], writes=[tmpD_buf[j1]])
                        tt_op(tmpD[j2][:, 0:NT], psum[bYB][:, 0:NT], tmpA[i2][:, 0:NT], ALU.mult,
                              reads=[psum_buf[bYB], tmpA_buf[i2]], writes=[tmpD_buf[j2]])
                        tt_op(r_(arena[:, 16 + f, tcols(t)]), tmpD[j1][:, 0:NT], tmpD[j2][:, 0:NT], ALU.add,
                              reads=[tmpD_buf[j1], tmpD_buf[j2]], writes=[a_buf[16 + f][t]])
            dummy_act(AF.Sqrt)
            for fp in range(2):
                sl = next_unit("WO")
                w4 = slot4(sl)
                if fp == 0 and KMAJOR:
                    def wo_first(w4, sl, t):
                        bk = {(g, sub): bank() for g in range(2) for sub in range(2)}
                        for kb in range(NCH):
                            def fn(eng, kb=kb, bk=bk, t=t):
                                ins = None
                                for g in range(2):
                                    for sub in range(2):
                                        ins = eng.matmul(psum[bk[(g, sub)]][:, 0:NT],
                                                         lhsT=w4[:, g, kb, sub * 128:(sub + 1) * 128],
                                                         rhs=r_(arena[:, 16 + kb, tcols(t)]),
                                                         start=(kb == 0), stop=(kb == NCH - 1))
                                return ins
                            P.op("pe", fn, reads=[slot_buf[sl], a_buf[16 + kb][t]],
                                 writes=[psum_buf[b_] for b_ in bk.values()])
                        for g in range(2):
                            for sub in range(2):
                                f = 2 * g + sub
                                bO = bk[(g, sub)]
                                tt_op(xT[:, f, tcols(t)], psum[bO][:, 0:NT], xT[:, f, tcols(t)], ALU.add,
                                      reads=[psum_buf[bO], x_buf[f][t]], writes=[x_buf[f][t]])
                    wo_first(w4, sl, 0)
                    wo_first(w4, sl, 1)
                    continue
                for g in range(2):
                    for sub in range(2):
                        f = 4 * fp + 2 * g + sub
                        for t in range(2):
                            bO = bank()
                            mm_group(psum[bO][:, 0:NT],
                                     [(w4[:, g, kb, sub * 128:(sub + 1) * 128], r_(arena[:, 16 + kb, tcols(t)]))
                                      for kb in range(NCH)],
                                     reads=[slot_buf[sl]] + [a_buf[16 + kb][t] for kb in range(NCH)], wbuf=psum_buf[bO])
                            tt_op(xT[:, f, tcols(t)], psum[bO][:, 0:NT], xT[:, f, tcols(t)], ALU.add,
                                  reads=[psum_buf[bO], x_buf[f][t]], writes=[x_buf[f][t]])

        def store_y(s):
            for (kind, r0, nb, c0) in io_blocks(s):
                sg = rr("stage", 2)
                for g in range(2):
                    b = bank()
                    rb = [x_buf[4 * g + i][t] for i in range(4) for t in tiles_of(c0, nb)]
                    transpose_group([(psum[b][0:nb, i * 128:(i + 1) * 128], xT[:, 4 * g + i, c0:c0 + nb], ident[:, :])
                                     for i in range(4)], reads=rb + [c_ident], wbuf=psum_buf[b])
                    copy_op("act" if g == 0 else "dve", stage[sg][0:nb, g * 512:(g + 1) * 512], psum[b][0:nb, :],
                            reads=[psum_buf[b]], writes=[stage_buf[sg]])
                if kind == "P":
                    dst = y_p[r0:r0 + nb, :]
                else:
                    o = 0 if kind == "S0" else 64
                    dst = y_s[o + r0:o + r0 + nb, :]
                out_toks.append(P.dma("sp", stage_st[sg], dst, stage[sg][0:nb, :], reads=[stage_buf[sg]]))

        def dump():
            allb = [b_ for l in x_buf for b_ in l] + [b_ for l in h_buf for b_ in l] + [b_ for l in a_buf for b_ in l]
            out_toks.append(P.dma("sp", misc_sem, dbg_x, xT[:, :, :].rearrange("p c t -> p (c t)"), reads=allb))
            out_toks.append(P.dma("sp", misc_sem, dbg_h, hT[:, :, :].rearrange("p c t -> p (c t)"), reads=allb))
            out_toks.append(P.dma("sp", misc_sem, dbg_a, arena[:, :, :].rearrange("p c t -> p (c t)"), reads=allb))

        phases = []
        for s in range(NST):
            phases += [("load", lambda s=s: load_x(s, prefetched=((0, 1, 2) if (s > 0 and dbg is None) else ()))),
                       ("norm1", lambda: rms_norm(R_FFN1, False, AF.Silu)),
                       ("ffn1", lambda: ffn(0)), ("norm2", lambda: rms_norm(R_MIX, False, AF.Sigmoid)),
                       ("mixer", lambda s=s: mixer(s)), ("norm3", lambda: rms_norm(R_FFN2, False, AF.Silu)),
                       ("ffn2", lambda s=s: ffn(1, after_gu=((lambda: prefetch_x(s + 1))
                                                             if (s + 1 < NST and dbg is None) else None))),
                       ("norm4", lambda s=s: (rms_norm(R_FIN, True),
                                              prefetch_x(s + 1, ks=(2,)) if (s + 1 < NST and dbg is None) else None)),
                       ("store", lambda s=s: store_y(s))]
        for pi, (name, fn_) in enumerate(phases):
            fn_()
            if dbg is not None and dbg == (pi // 9, name):
                dump()
                break
        if dbg is None:
            assert ustate["next"] == len(units)
        P.wait_all("sp", out_toks)

        with nc.Block() as block:
            @block.sync
            def _(eng):
                for f in P.streams["sp"]:
                    f(eng)

            @block.gpsimd
            def _(eng):
                for f in P.streams["pool"]:
                    f(eng)

            @block.tensor
            def _(eng):
                for f in P.streams["pe"]:
                    f(eng)

            @block.vector
            def _(eng):
                for f in P.streams["dve"]:
                    f(eng)

            @block.scalar
            def _(eng):
                for f in P.streams["act"]:
                    f(eng)
    return nc


_NC_CACHE = {}


def kernel(x_prompt, x_sample, cache_conv_a, cache_conv_b, ffn1_norm, ffn1_w_gate_up, ffn1_w_down, mix_norm,
           w_in, conv_a_w, conv_b_w, conv_b_bias, conv_b_ln_g, conv_b_ln_b, w_a_out, w_b_out, w_out,
           ffn2_norm, ffn2_w_gate_up, ffn2_w_down, final_norm):
    f32 = np.float32
    A = lambda a: np.ascontiguousarray(np.asarray(a, dtype=f32))
    n = 8
    if "nc" not in _NC_CACHE:
        _NC_CACHE["nc"] = build_nc()
    nc = _NC_CACHE["nc"]
    cst = np.concatenate([A(ffn1_norm).reshape(1, D), A(mix_norm).reshape(1, D), A(ffn2_norm).reshape(1, D),
                          A(final_norm).reshape(1, D), A(conv_b_bias).reshape(1, D), A(conv_b_ln_g).reshape(1, D),
                          A(conv_b_ln_b).reshape(1, D), A(conv_a_w).reshape(3, D), A(conv_b_w).reshape(31, D)], axis=0)
    shared = {
        "cst": np.ascontiguousarray(cst),
        "ident_in": np.eye(128, dtype=f32),
        "w_gu1": A(ffn1_w_gate_up)[0], "w_gu2": A(ffn2_w_gate_up)[0],
        "w_dn1": A(ffn1_w_down)[0], "w_dn2": A(ffn2_w_down)[0],
        "w_in": A(w_in)[0], "w_ao": A(w_a_out)[0], "w_bo": A(w_b_out)[0], "w_o": A(w_out)[0],
    }
    xp, xs = A(x_prompt), A(x_sample)
    ca, cb = A(cache_conv_a)[0], A(cache_conv_b)[0]
    in_maps = []
    for i in range(n):
        m = dict(shared)
        m["x_p"] = xp[i]
        m["x_s"] = np.ascontiguousarray(xs[2 * i:2 * i + 2].reshape(128, D))
        m["hist"] = np.ascontiguousarray(np.concatenate([cb[2 * i], cb[2 * i + 1], ca[2 * i], ca[2 * i + 1]], axis=0))
        in_maps.append(m)
    res = run_bass_kernel_spmd(nc, in_maps, core_ids=list(range(n)))
    R = res.results
    y_prompt = np.stack([R[i]["y_p"] for i in range(n)], axis=0).astype(f32)
    y_sample = np.concatenate([R[i]["y_s"].reshape(2, 64, D) for i in range(n)], axis=0).astype(f32)
    nca_p = np.stack([R[i]["nca_p"] for i in range(n)], axis=0)[None].astype(f32)
    ncb_p = np.stack([R[i]["ncb_p"] for i in range(n)], axis=0)[None].astype(f32)
    nca_s = np.concatenate([R[i]["nca_s"].reshape(2, 2, D) for i in range(n)], axis=0)[None].astype(f32)
    ncb_s = np.concatenate([R[i]["ncb_s"].reshape(2, 30, D) for i in range(n)], axis=0)[None].astype(f32)
    return (y_prompt, y_sample, nca_p, ncb_p, nca_s, ncb_s)
```

```python
import numpy as np
import concourse.bass as bass
import concourse.mybir as mybir
from concourse.bass_utils import run_bass_kernel_spmd

F32 = mybir.dt.float32
F32R = mybir.dt.float32r
ALU = mybir.AluOpType
AF = mybir.ActivationFunctionType

D = 1024
DFF = 2816
NCH = 8
NJ = 22
TS = 544
NT = 272
NST = 4
HB = 30
HA = 2
EPS = 1e-6
GXW = 432
CXW = 344
NDIAG = 16
NSLOT = 4
XPOSE_MM = True
KMAJOR = True
LN_SPLIT = True
DIAG_ENG = "dve"
CG = 8
R_FFN1, R_MIX, R_FFN2, R_FIN, R_CBB, R_LNG, R_LNB, R_CAW, R_CBW = 0, 1, 2, 3, 4, 5, 6, 7, 10
NCR = 41


class Buf:
    __slots__ = ("lw", "rd")

    def __init__(self):
        self.lw = None
        self.rd = []


class Prog:
    ENG = ("pe", "act", "dve", "pool", "sp")

    def __init__(self):
        self.streams = {e: [] for e in self.ENG}
        self.cnt = {e: 0 for e in self.ENG}
        self.waited = {e: {} for e in self.ENG}
        self.dcnt = {}
        self.sems = {}

    def _deps(self, engine, reads, writes):
        need = {}
        for b in reads:
            if b.lw is not None:
                k, v = b.lw
                need[k] = max(need.get(k, 0), v)
        for b in writes:
            if b.lw is not None:
                k, v = b.lw
                need[k] = max(need.get(k, 0), v)
            for (k, v) in b.rd:
                need[k] = max(need.get(k, 0), v)
        w = self.waited[engine]
        waits = []
        if engine == "pe":
            need.pop("pe", None)
        for k, v in need.items():
            if w.get(k, 0) < v:
                waits.append((k, v))
                w[k] = v
        return waits

    def _mark(self, tok, reads, writes):
        for b in reads:
            b.rd.append(tok)
        for b in writes:
            b.lw = tok
            b.rd = []

    def op(self, engine, fn, reads=(), writes=()):
        waits = self._deps(engine, reads, writes)
        self.cnt[engine] += 1
        tok = (engine, self.cnt[engine])
        sems = self.sems

        def emit(eng):
            for (k, v) in waits:
                eng.wait_ge(sems[k], v)
            ins = fn(eng)
            ins.then_inc(sems[engine], 1)
        self.streams[engine].append(emit)
        self._mark(tok, reads, writes)
        return tok

    def dma(self, queue, dsem, out_ap, in_ap, reads=(), writes=(), cont=False):
        waits = self._deps(queue, reads, writes)
        if cont:
            waits = [(k, v) for (k, v) in waits if k != dsem]
        self.dcnt[dsem] = self.dcnt.get(dsem, 0) + 16
        tok = (dsem, self.dcnt[dsem])
        sems = self.sems

        def emit(eng):
            for (k, v) in waits:
                eng.wait_ge(sems[k], v)
            eng.dma_start(out=out_ap, in_=in_ap).then_inc(sems[dsem], 16)
        self.streams[queue].append(emit)
        self._mark(tok, reads, writes)
        return tok

    def wait_all(self, engine, toks):
        need = {}
        for (k, v) in toks:
            need[k] = max(need.get(k, 0), v)
        sems = self.sems
        lst = list(need.items())

        def emit(eng):
            for (k, v) in lst:
                eng.wait_ge(sems[k], v)
        self.streams[engine].append(emit)


def st_segments(s):
    if s < 3:
        return [("P", TS * s, TS, 0)]
    return [("P", 1632, 416, 0), ("S0", 0, 64, 416), ("S1", 0, 64, 480)]


def tile_pieces(s, t):
    out = []
    lo, hi = NT * t, NT * (t + 1)
    for (kind, r0, ln, c0) in st_segments(s):
        a, b = max(lo, c0), min(hi, c0 + ln)
        if a < b:
            out.append((kind, r0 + (a - c0), b - a, a - lo))
    return out


def ext_layout(pieces, H):
    starts = []
    pos = 0
    for (_, _, ln, _) in pieces:
        starts.append(pos + H)
        pos += H + ln
    return starts, pos


def io_blocks(s):
    out = []
    for (kind, r0, ln, c0) in st_segments(s):
        o = 0
        while o < ln:
            nb = min(128, ln - o)
            out.append((kind, r0 + o, nb, c0 + o))
            o += nb
    return out


def build_nc(dbg=None):
    nc = bass.Bass("TRN2", target_bir_lowering=False)

    def din(name, shape):
        return nc.dram_tensor(name, shape, F32, kind="ExternalInput").ap()

    def dout(name, shape):
        return nc.dram_tensor(name, shape, F32, kind="ExternalOutput").ap()

    x_p = din("x_p", [2048, D])
    x_s = din("x_s", [128, D])
    hist_in = din("hist", [64, D])
    cst_in = din("cst", [NCR, D])
    ident_in = din("ident_in", [128, 128])
    w_gu = [din("w_gu1", [D, 2 * DFF]), din("w_gu2", [D, 2 * DFF])]
    w_dn = [din("w_dn1", [DFF, D]), din("w_dn2", [DFF, D])]
    w_in = din("w_in", [D, 7 * D])
    w_ao = din("w_ao", [D, D])
    w_bo = din("w_bo", [D, D])
    w_o = din("w_o", [D, D])
    y_p = dout("y_p", [2048, D])
    y_s = dout("y_s", [128, D])
    nca_p = dout("nca_p", [2, D])
    ncb_p = dout("ncb_p", [30, D])
    nca_s = dout("nca_s", [4, D])
    ncb_s = dout("ncb_s", [60, D])
    if dbg is not None:
        dbg_x = dout("dbg_x", [128, NCH * TS])
        dbg_h = dout("dbg_h", [128, NCH * TS])
        dbg_a = dout("dbg_a", [128, 24 * TS])

    def wv(w):
        return w.rearrange("(kb p) n -> p kb n", p=128)

    w_gu_v = [wv(w) for w in w_gu]
    w_dn_v = [wv(w) for w in w_dn]
    w_in_v = wv(w_in)
    w_ao_v, w_bo_v, w_o_v = wv(w_ao), wv(w_bo), wv(w_o)

    P = Prog()
    import contextlib
    with contextlib.ExitStack() as es:
        def sb(name, shape, dt=F32):
            return es.enter_context(nc.sbuf_tensor(name, shape, dt))

        def sem(name):
            s_ = es.enter_context(nc.semaphore(name))
            P.sems[name] = s_
            return name

        for e in Prog.ENG:
            sem(e)

        xT = sb("xT", [128, NCH, TS])
        hT = sb("hT", [128, NCH, TS])
        arena = sb("arena", [128, 24, TS])
        slots = [sb("wslot%d" % i, [128, 4096], F32R) for i in range(NSLOT)]
        slot_sem = [sem("wsl%d" % i) for i in range(NSLOT)]
        slot_buf = [Buf() for _ in range(NSLOT)]
        diag = sb("diag", [128, NDIAG, 128], F32R)
        diag_buf = [Buf() for _ in range(NDIAG)]
        diagA = sb("diagA", [128, 6, 128], F32R)
        diagA_buf = [Buf() for _ in range(6)]
        gx = [[sb("gx%d%d" % (a, b), [128, GXW]) for b in range(2)] for a in range(2)]
        gx_buf = [[Buf() for _ in range(2)] for _ in range(2)]
        cx = [[sb("cx%d%d" % (a, b), [128, CXW]) for b in range(2)] for a in range(2)]
        cx_buf = [[Buf() for _ in range(2)] for _ in range(2)]
        NTA, NTD, NSTT = 6, 4, 6
        tmpA_all = sb("tmpA_all", [128, NTA, NT])
        tmpA = [tmpA_all[:, i, :] for i in range(NTA)]
        tmpA_buf = [Buf() for _ in range(NTA)]
        tmpD_all = sb("tmpD_all", [128, NTD, NT])
        tmpD = [tmpD_all[:, i, :] for i in range(NTD)]
        tmpD_buf = [Buf() for _ in range(NTD)]
        NTQ = 3
        tmpQ = [sb("tmpQ%d" % i, [128, NT], F32R) for i in range(NTQ)]
        tmpQ_buf = [Buf() for _ in range(NTQ)]
        stt_all = sb("stt_all", [128, NSTT, NT])
        stt = [stt_all[:, i, :] for i in range(NSTT)]
        stt_buf = [Buf() for _ in range(NSTT)]
        stage = [sb("stage%d" % i, [128, D]) for i in range(2)]
        stage_buf = [Buf() for _ in range(2)]
        stage_ld = [sem("stld%d" % i) for i in range(2)]
        LD = [tmpA_all[:, 0:4, :].rearrange("p a n -> p (a n)"), tmpD_all[:, 0:4, :].rearrange("p a n -> p (a n)"),
              stt_all[:, 0:4, :].rearrange("p a n -> p (a n)")]
        LD_buf = [tmpA_buf[0:4], tmpD_buf[0:4], stt_buf[0:4]]
        ld_sem = [sem("xld%d" % i) for i in range(3)]
        ld_sem_sw = [sem("xlds%d" % i) for i in range(3)]
        stage_st = [sem("stst%d" % i) for i in range(2)]
        ident = sb("ident", [128, 128])
        ones = sb("ones", [128, 128], F32R)
        ones_f = sb("ones_f", [128, 128])
        c_onesf = Buf()
        epst = sb("epst", [128, 1])
        dmy = sb("dmy", [128, 1])
        c_dmy = Buf()
        cvec = sb("cvec", [128, NCH, NCR])
        hist = sb("hist_sb", [128, NCH, 64])
        carryB = sb("carryB", [128, NCH, HB])
        carryA = sb("carryA", [128, NCH, HA])
        gatB = sb("gatB", [128, 3 * HB])
        gatA = sb("gatA", [128, 3 * HA])
        c_ident, c_ones, c_eps, c_cvec, c_hist = Buf(), Buf(), Buf(), Buf(), Buf()
        c_carryB = [Buf() for _ in range(NCH)]
        c_carryA = [Buf() for _ in range(NCH)]
        c_gatB, c_gatA = Buf(), Buf()
        psum = [es.enter_context(nc.psum_tensor("ps%d" % i, [128, 512], F32)) for i in range(8)]
        psum_buf = [Buf() for _ in range(8)]
        x_buf = [[Buf() for _ in range(2)] for _ in range(NCH)]
        h_buf = [[Buf() for _ in range(2)] for _ in range(NCH)]
        a_buf = [[Buf() for _ in range(2)] for _ in range(24)]
        misc_sem = sem("misc")
        out_toks = []

        state = {"bank": 0, "diagA": 0, "tq": 0, "ta": 0, "td": 0, "st": 0, "slot": 0, "diag": 0, "stage": 0}

        def rr(key, n):
            v = state[key]
            state[key] = (v + 1) % n
            return v

        def bank():
            return rr("bank", 8)

        def tA():
            return rr("ta", NTA)

        def tD():
            return rr("td", NTD)

        def tQ():
            return rr("tq", NTQ)

        def tS():
            return rr("st", NSTT)

        def r_(ap):
            return ap.bitcast(F32R)

        def tcols(t, a=0, b=NT):
            return slice(NT * t + a, NT * t + b)

        def mm_group(out_ap, pairs, reads, wbuf, first=True, last=True):
            n = len(pairs)

            def fn(eng):
                ins = None
                for i, (l, r) in enumerate(pairs):
                    ins = eng.matmul(out_ap, lhsT=l, rhs=r, start=(first and i == 0),
                                     stop=(last and i == n - 1))
                return ins
            return P.op("pe", fn, reads=reads, writes=[wbuf])

        def act_op(out_ap, in_ap, func, reads, writes, bias=None, scale=None):
            def fn(eng):
                kw = {}
                if bias is not None:
                    kw["bias"] = bias
                if scale is not None:
                    kw["scale"] = scale
                return eng.activation(out=out_ap, in_=in_ap, func=func, **kw)
            return P.op("act", fn, reads=reads, writes=writes)

        def dummy_act(func):
            return act_op(dmy[:, 0:1], epst[:, 0:1], func, reads=[c_eps], writes=[c_dmy])

        def tt_op(out_ap, in0, in1, op, reads, writes, engine="dve"):
            def fn(eng):
                return eng.tensor_tensor(out=out_ap, in0=in0, in1=in1, op=op)
            return P.op(engine, fn, reads=reads, writes=writes)

        def stt_op(out_ap, in0, scalar, in1, op0, op1, reads, writes):
            def fn(eng):
                return eng.scalar_tensor_tensor(out=out_ap, in0=in0, scalar=scalar, in1=in1, op0=op0, op1=op1)
            return P.op("dve", fn, reads=reads, writes=writes)

        def ts_op(out_ap, in0, s1, s2, op0, op1, reads, writes, engine="dve"):
            def fn(eng):
                return eng.tensor_scalar(out=out_ap, in0=in0, scalar1=s1, scalar2=s2, op0=op0, op1=op1)
            return P.op(engine, fn, reads=reads, writes=writes)

        def copy_op(engine, out_ap, in_ap, reads, writes):
            def fn(eng):
                if engine == "act":
                    return eng.copy(out=out_ap, in_=in_ap)
                return eng.tensor_copy(out=out_ap, in_=in_ap)
            return P.op(engine, fn, reads=reads, writes=writes)

        def transpose_group(items, reads, wbuf):
            def fn(eng):
                ins = None
                for (o, i_, idn) in items:
                    if XPOSE_MM:
                        ins = eng.matmul(o, lhsT=i_, rhs=idn, start=True, stop=True)
                    else:
                        ins = eng.transpose(out=o, in_=i_, identity=idn)
                return ins
            return P.op("pe", fn, reads=reads, writes=[wbuf])

        units = []
        for s in range(NST):
            for which in range(2):
                if which == 1:
                    for cc in range(4):
                        units.append(("CV", [(w_in_v, 0, 8, 1024 + cc * 256), (w_in_v, 0, 8, 2048 + cc * 256)]))
                        units.append(("AB", [(w_in_v, 0, 8, cc * 256)]))
                        units.append(("UU", [(w_in_v, 0, 8, 3072 + cc * 256), (w_in_v, 0, 8, 4096 + cc * 256)]))
                    for ff in range(4):
                        units.append(("GG", [(w_in_v, 0, 8, 5120 + ff * 256), (w_in_v, 0, 8, 6144 + ff * 256)]))
                        units.append(("AO", [(w_ao_v, 0, 8, ff * 256), (w_bo_v, 0, 8, ff * 256)]))
                    for fp in range(2):
                        units.append(("WO", [(w_o_v, 0, 8, fp * 512), (w_o_v, 0, 8, fp * 512 + 256)]))
                for jj in range(11):
                    units.append(("GU", [(w_gu_v[which], 0, 8, jj * 256), (w_gu_v[which], 0, 8, DFF + jj * 256)]))
                for ff in range(4):
                    for half in range(2):
                        units.append(("DN", [(w_dn_v[which], 11 * half, 11, ff * 256)]))
        ustate = {"next": 0, "issued": 0}

        def issue_load(u):
            tag, halves = units[u]
            sl = u % NSLOT
            for hi, (wview, kb0, nkb, col0) in enumerate(halves):
                off = hi * 2048
                o = slots[sl][:, off:off + nkb * 256].rearrange("p (k n) -> p k n", n=256)
                i_ = wview[:, kb0:kb0 + nkb, col0:col0 + 256]
                P.dma("pool", slot_sem[sl], o, i_, writes=[slot_buf[sl]], cont=(hi > 0))

        def next_unit(tag, hold=0):
            u = ustate["next"]
            assert units[u][0] == tag, (units[u][0], tag)
            while ustate["issued"] < min(len(units), u + NSLOT - hold):
                issue_load(ustate["issued"])
                ustate["issued"] += 1
            ustate["next"] = u + 1
            return u % NSLOT

        def slot4(sl):
            return slots[sl][:, :].rearrange("p (h k n) -> p h k n", h=2, k=8, n=256)

        def slot_half(sl, hi):
            return slots[sl][:, hi * 2048:(hi + 1) * 2048].rearrange("p (k n) -> p k n", n=256)

        P.dma("sp", misc_sem, ident[:, :], ident_in, writes=[c_ident])
        P.op("pool", lambda eng: eng.memset(ones_f[:, :], 1.0), writes=[c_onesf])
        copy_op("pool", ones[:, :], ones_f[:, :], reads=[c_onesf], writes=[c_ones])
        P.op("pool", lambda eng: eng.memset(epst[:, :], EPS), writes=[c_eps])
        P.op("pool", lambda eng: eng.memset(carryB[:, :, :], 0.0), writes=c_carryB)
        P.op("pool", lambda eng: eng.memset(carryA[:, :, :], 0.0), writes=c_carryA)
        P.dma("sp", stage_ld[0], stage[0][0:NCR, :], cst_in, writes=[stage_buf[0]])
        P.dma("sp", stage_ld[1], stage[1][0:64, :], hist_in, writes=[stage_buf[1]])
        b = bank()
        transpose_group([(psum[b][:, c * 64:c * 64 + NCR], stage[0][0:NCR, c * 128:(c + 1) * 128], ident[0:NCR, 0:NCR])
                         for c in range(NCH)], reads=[stage_buf[0], c_ident], wbuf=psum_buf[b])
        copy_op("dve", cvec[:, :, :], psum[b][:, :].rearrange("p (c w) -> p c w", w=64)[:, :, 0:NCR],
                reads=[psum_buf[b]], writes=[c_cvec])
        b = bank()
        transpose_group([(psum[b][:, c * 64:(c + 1) * 64], stage[1][0:64, c * 128:(c + 1) * 128], ident[0:64, 0:64])
                         for c in range(NCH)], reads=[stage_buf[1], c_ident], wbuf=psum_buf[b])
        copy_op("act", hist[:, :, :], psum[b][:, :].rearrange("p (c w) -> p c w", w=64),
                reads=[psum_buf[b]], writes=[c_hist])

        def tiles_of(c0, nb):
            return [t for t in range(2) if c0 < NT * (t + 1) and c0 + nb > NT * t]

        def x_src(kind, r0, nb):
            if kind == "P":
                return x_p[r0:r0 + nb, :]
            o = 0 if kind == "S0" else 64
            return x_s[o + r0:o + r0 + nb, :]

        def prefetch_x(s, ks=(0, 1)):
            blocks = io_blocks(s)
            for k in ks:
                (kind, r0, nb, c0) = blocks[k]
                P.dma("sp", ld_sem[k % 3], LD[k % 3][0:nb, 0:D], x_src(kind, r0, nb), writes=LD_buf[k % 3])

        def load_x(s, prefetched=()):
            for k, (kind, r0, nb, c0) in enumerate(io_blocks(s)):
                sg = k % 3
                if k not in prefetched:
                    q = "pool" if prefetched else "sp"
                    P.dma(q, (ld_sem_sw if q == "pool" else ld_sem)[sg], LD[sg][0:nb, 0:D], x_src(kind, r0, nb),
                          writes=LD_buf[sg])
                for g in range(2):
                    b = bank()
                    transpose_group([(psum[b][:, i * 128:i * 128 + nb],
                                      LD[sg][0:nb, (4 * g + i) * 128:(4 * g + i + 1) * 128],
                                      ident[0:nb, 0:nb]) for i in range(4)],
                                    reads=LD_buf[sg] + [c_ident], wbuf=psum_buf[b])
                    wb = [x_buf[4 * g + i][t] for i in range(4) for t in tiles_of(c0, nb)]
                    copy_op("act" if g == 0 else "dve", xT[:, 4 * g:4 * g + 4, c0:c0 + nb],
                            psum[b][:, :].rearrange("p (i w) -> p i w", w=128)[:, :, 0:nb],
                            reads=[psum_buf[b]], writes=wb)

        def rms_norm(row, inplace, nxt=None):
            for t in range(2):
                b = bank()
                for f in range(NCH):
                    i = tQ()
                    if True:
                        act_op(tmpQ[i][:, :], xT[:, f, tcols(t)], AF.Square,
                               reads=[x_buf[f][t]], writes=[tmpQ_buf[i]])
                    else:
                        tt_op(tmpQ[i][:, :], xT[:, f, tcols(t)], xT[:, f, tcols(t)], ALU.mult,
                              reads=[x_buf[f][t]], writes=[tmpQ_buf[i]])
                    mm_group(psum[b][:, 0:NT], [(ones[:, :], tmpQ[i][:, :])],
                             reads=[tmpQ_buf[i], c_ones], wbuf=psum_buf[b], first=(f == 0), last=(f == NCH - 1))
                i1, i2 = tS(), tS()
                act_op(stt[i1][:, :], psum[b][:, 0:NT], AF.Sqrt, reads=[psum_buf[b], c_eps], writes=[stt_buf[i1]],
                       bias=epst[:, 0:1], scale=1.0 / D)
                if t == 1 and nxt is not None:
                    dummy_act(nxt)

                def fn(eng, i1=i1, i2=i2):
                    return eng.reciprocal(out=stt[i2][:, :], in_=stt[i1][:, :])
                P.op("dve", fn, reads=[stt_buf[i1]], writes=[stt_buf[i2]])
                for f in range(NCH):
                    if inplace:
                        o, wb = xT[:, f, tcols(t)], [x_buf[f][t]]
                    else:
                        o, wb = r_(hT[:, f, tcols(t)]), [h_buf[f][t]]
                    stt_op(o, xT[:, f, tcols(t)], cvec[:, f, row:row + 1], stt[i2][:, :], ALU.mult, ALU.mult,
                           reads=[x_buf[f][t], stt_buf[i2], c_cvec], writes=wb)

        def kmajor_first(w4, sl, t):
            bk = {(hf, sub): bank() for sub in range(2) for hf in range(2)}
            for kb in range(NCH):
                def fn(eng, kb=kb, bk=bk, t=t):
                    ins = None
                    for sub in range(2):
                        for hf in range(2):
                            ins = eng.matmul(psum[bk[(hf, sub)]][:, 0:NT], lhsT=w4[:, hf, kb, sub * 128:(sub + 1) * 128],
                                             rhs=r_(hT[:, kb, tcols(t)]), start=(kb == 0), stop=(kb == NCH - 1))
                    return ins
                P.op("pe", fn, reads=[slot_buf[sl], h_buf[kb][t]], writes=[psum_buf[b_] for b_ in bk.values()])
            return bk

        def ffn(which, after_gu=None):
            for jj in range(11):
                sl = next_unit("GU")
                w4 = slot4(sl)
                combos = [(sub, t) for sub in range(2) for t in range(2)]
                if jj == 0:
                    combos = [(sub, t) for t in range(2) for sub in range(2)]
                pre = {}
                for (sub, t) in combos:
                    j = 2 * jj + sub
                    if jj == 0 and KMAJOR:
                        if t not in pre:
                            pre[t] = kmajor_first(w4, sl, t)
                        bG, bU = pre[t][(0, sub)], pre[t][(1, sub)]
                    else:
                        bG, bU = bank(), bank()
                        hr = [h_buf[kb][t] for kb in range(NCH)]
                        mm_group(psum[bG][:, 0:NT], [(w4[:, 0, kb, sub * 128:(sub + 1) * 128], r_(hT[:, kb, tcols(t)]))
                                                     for kb in range(NCH)], reads=[slot_buf[sl]] + hr, wbuf=psum_buf[bG])
                        mm_group(psum[bU][:, 0:NT], [(w4[:, 1, kb, sub * 128:(sub + 1) * 128], r_(hT[:, kb, tcols(t)]))
                                                     for kb in range(NCH)], reads=[slot_buf[sl]] + hr, wbuf=psum_buf[bU])
                    if True:
                        i = tA()
                        act_op(tmpA[i][:, 0:NT], psum[bG][:, 0:NT], AF.Silu, reads=[psum_buf[bG]], writes=[tmpA_buf[i]])
                        tt_op(r_(arena[:, j, tcols(t)]), psum[bU][:, 0:NT], tmpA[i][:, 0:NT], ALU.mult,
                              reads=[psum_buf[bU], tmpA_buf[i]], writes=[a_buf[j][t]])
            dummy_act(AF.Sqrt)
            if after_gu is not None:
                after_gu()
            for ff in range(4):
                bks = [[bank() for t in range(2)] for sub in range(2)]
                for half in range(2):
                    sl = next_unit("DN")
                    wd = slots[sl][:, 0:11 * 256].rearrange("p (k n) -> p k n", n=256)
                    for sub in range(2):
                        for t in range(2):
                            bO = bks[sub][t]
                            mm_group(psum[bO][:, 0:NT],
                                     [(wd[:, kk, sub * 128:(sub + 1) * 128], r_(arena[:, 11 * half + kk, tcols(t)]))
                                      for kk in range(11)],
                                     reads=[slot_buf[sl]] + [a_buf[11 * half + kk][t] for kk in range(11)],
                                     wbuf=psum_buf[bO], first=(half == 0), last=(half == 1))
                for sub in range(2):
                    f = 2 * ff + sub
                    for t in range(2):
                        bO = bks[sub][t]
                        stt_op(xT[:, f, tcols(t)], psum[bO][:, 0:NT], 0.5, xT[:, f, tcols(t)], ALU.mult, ALU.add,
                               reads=[psum_buf[bO], x_buf[f][t]], writes=[x_buf[f][t]])

        def build_diag(c, row):
            d = rr("diag", NDIAG)
            ts_op(r_(diag[:, d, :]), ident[:, :], cvec[:, c, row:row + 1], 0.0, ALU.mult, ALU.add,
                  reads=[c_ident, c_cvec, ], writes=[diag_buf[d]], engine=DIAG_ENG)
            return d

        def build_diagA(c, row):
            d = rr("diagA", 6)
            ts_op(r_(diagA[:, d, :]), ident[:, :], cvec[:, c, row:row + 1], 0.0, ALU.mult, ALU.add,
                  reads=[c_ident, c_cvec, ], writes=[diagA_buf[d]], engine=DIAG_ENG)
            return d

        def halo_src(s, t, sub, c, kind, row0, which):
            H = HB if which == "B" else HA
            if kind == "P":
                if t == 0:
                    cr = carryB if which == "B" else carryA
                    cb_ = c_carryB if which == "B" else c_carryA
                    return cr[:, c, :], [cb_[c]]
                ext = gx if which == "B" else cx
                eb = gx_buf if which == "B" else cx_buf
                return ext[sub][0][:, NT:NT + H], [eb[sub][0]]
            if which == "B":
                o = 0 if kind == "S0" else 30
                return hist[:, c, o:o + 30], [c_hist]
            o = 60 if kind == "S0" else 62
            return hist[:, c, o:o + 2], [c_hist]

        def mixer(s):
            pcs = [tile_pieces(s, t) for t in range(2)]
            layA = [ext_layout(pcs[t], HA) for t in range(2)]
            layB = [ext_layout(pcs[t], HB) for t in range(2)]
            pend = {}

            def conv_b_pre(cc):
                pend[cc] = [build_diag(2 * cc, R_CBW + k) for k in range(CG)]

            def conv_b(cc):
                for sub in range(2):
                    c = 2 * cc + sub
                    bb = [bank(), bank()]
                    for k0 in range(0, HB + 1, CG):
                        ks = list(range(k0, min(HB + 1, k0 + CG)))
                        if sub == 0 and k0 == 0 and cc in pend:
                            ds = pend.pop(cc)
                        else:
                            ds = [build_diag(c, R_CBW + k) for k in ks]
                        for t in range(2):
                            nconv = layB[t][1] - HB
                            mm_group(psum[bb[t]][:, 0:nconv],
                                     [(diag[:, ds[i], :], r_(gx[sub][t][:, k:k + nconv])) for i, k in enumerate(ks)],
                                     reads=[diag_buf[d_] for d_ in ds] + [gx_buf[sub][t]], wbuf=psum_buf[bb[t]],
                                     first=(k0 == 0), last=(ks[-1] == HB))
                    for t in range(2):
                        for pi, (kind, row0, ln, tc0) in enumerate(pcs[t]):
                            es_ = layB[t][0][pi]
                            act_op(r_(arena[:, 8 + c, tcols(t, tc0, tc0 + ln)]), psum[bb[t]][:, es_ - HB:es_ - HB + ln],
                                   AF.Identity, reads=[psum_buf[bb[t]], c_cvec], writes=[a_buf[8 + c][t]],
                                   bias=cvec[:, c, R_CBB:R_CBB + 1])

            for cc in range(4):
                if cc > 0:
                    conv_b_pre(cc - 1)
                dsA = [[build_diagA(2 * cc + sub, R_CAW + k) for k in range(HA + 1)] for sub in range(2)]
                sl = next_unit("CV")
                w4 = slot4(sl)
                combos = [(sub, t) for sub in range(2) for t in range(2)]
                if cc == 0:
                    combos = [(sub, t) for t in range(2) for sub in range(2)]
                pre = {}
                for (sub, t) in combos:
                    c = 2 * cc + sub
                    if cc == 0 and KMAJOR:
                        if t not in pre:
                            pre[t] = kmajor_first(w4, sl, t)
                        bC, bV = pre[t][(0, sub)], pre[t][(1, sub)]
                    else:
                        bC, bV = bank(), bank()
                        hr = [h_buf[kb][t] for kb in range(NCH)]
                        mm_group(psum[bC][:, 0:NT], [(w4[:, 0, kb, sub * 128:(sub + 1) * 128], r_(hT[:, kb, tcols(t)]))
                                                     for kb in range(NCH)], reads=[slot_buf[sl]] + hr, wbuf=psum_buf[bC])
                        mm_group(psum[bV][:, 0:NT], [(w4[:, 1, kb, sub * 128:(sub + 1) * 128], r_(hT[:, kb, tcols(t)]))
                                                     for kb in range(NCH)], reads=[slot_buf[sl]] + hr, wbuf=psum_buf[bV])
                    if True:
                        i = tA()
                        copy_op("act", tmpA[i][:, 0:NT], psum[bC][:, 0:NT], reads=[psum_buf[bC]], writes=[tmpA_buf[i]])
                        for pi, (kind, row0, ln, tc0) in enumerate(pcs[t]):
                            es_ = layA[t][0][pi]
                            src, rb = halo_src(s, t, sub, c, kind, row0, "A")
                            copy_op("pool", r_(cx[sub][t][:, es_ - HA:es_]), src, reads=rb, writes=[cx_buf[sub][t]])
                            tt_op(r_(cx[sub][t][:, es_:es_ + ln]), psum[bV][:, tc0:tc0 + ln], tmpA[i][:, tc0:tc0 + ln],
                                  ALU.mult, reads=[psum_buf[bV], tmpA_buf[i]], writes=[cx_buf[sub][t]])
                        if t == 1 and s < NST - 1:
                            copy_op("pool", carryA[:, c, :], cx[sub][1][:, NT:NT + HA],
                                    reads=[cx_buf[sub][1]], writes=[c_carryA[c]])
                        if t == 1 and s == NST - 1:
                            copy_op("pool", gatA[:, :].rearrange("p (s w) -> p s w", w=HA),
                                    cx[sub][1][:, 144:144 + 3 * 66].rearrange("p (s w) -> p s w", w=66)[:, :, 0:HA],
                                    reads=[cx_buf[sub][1]], writes=[c_gatA])
                            b = bank()
                            transpose_group([(psum[b][0:3 * HA, 0:128], gatA[:, :], ident[:, :])],
                                            reads=[c_gatA, c_ident], wbuf=psum_buf[b])
                            copy_op("act", stage[1][0:3 * HA, c * 128:(c + 1) * 128], psum[b][0:3 * HA, 0:128],
                                    reads=[psum_buf[b]], writes=[stage_buf[1]])
                if cc > 0:
                    conv_b(cc - 1)
                sl = next_unit("AB")
                w3 = slot_half(sl, 0)
                tbs = {}
                for sub in range(2):
                    for t in range(2):
                        bB = bank()
                        hr = [h_buf[kb][t] for kb in range(NCH)]
                        mm_group(psum[bB][:, 0:NT], [(w3[:, kb, sub * 128:(sub + 1) * 128], r_(hT[:, kb, tcols(t)]))
                                                     for kb in range(NCH)], reads=[slot_buf[sl]] + hr, wbuf=psum_buf[bB])
                        i = tA()
                        copy_op("act", tmpA[i][:, 0:NT], psum[bB][:, 0:NT], reads=[psum_buf[bB]], writes=[tmpA_buf[i]])
                        tbs[(sub, t)] = i
                for sub in range(2):
                    c = 2 * cc + sub
                    ds = dsA[sub]
                    for t in range(2):
                        bA = bank()
                        nconv = layA[t][1] - HA
                        mm_group(psum[bA][:, 0:nconv], [(diagA[:, ds[k], :], r_(cx[sub][t][:, k:k + nconv]))
                                                        for k in range(HA + 1)],
                                 reads=[diagA_buf[d_] for d_ in ds] + [cx_buf[sub][t]], wbuf=psum_buf[bA])
                        i = tbs[(sub, t)]
                        for pi, (kind, row0, ln, tc0) in enumerate(pcs[t]):
                            es_ = layA[t][0][pi]
                            tt_op(r_(arena[:, c, tcols(t, tc0, tc0 + ln)]), psum[bA][:, es_ - HA:es_ - HA + ln],
                                  tmpA[i][:, tc0:tc0 + ln], ALU.mult, reads=[psum_buf[bA], tmpA_buf[i]],
                                  writes=[a_buf[c][t]])
                if cc == 3:
                    conv_b_pre(3)
                sl = next_unit("UU")
                w4 = slot4(sl)
                for sub in range(2):
                    c = 2 * cc + sub
                    for t in range(2):
                        b1, b2 = bank(), bank()
                        hr = [h_buf[kb][t] for kb in range(NCH)]
                        mm_group(psum[b1][:, 0:NT], [(w4[:, 0, kb, sub * 128:(sub + 1) * 128], r_(hT[:, kb, tcols(t)]))
                                                     for kb in range(NCH)], reads=[slot_buf[sl]] + hr, wbuf=psum_buf[b1])
                        mm_group(psum[b2][:, 0:NT], [(w4[:, 1, kb, sub * 128:(sub + 1) * 128], r_(hT[:, kb, tcols(t)]))
                                                     for kb in range(NCH)], reads=[slot_buf[sl]] + hr, wbuf=psum_buf[b2])
                        i = tA()
                        act_op(tmpA[i][:, 0:NT], psum[b2][:, 0:NT], AF.Sigmoid, reads=[psum_buf[b2]], writes=[tmpA_buf[i]])
                        for pi, (kind, row0, ln, tc0) in enumerate(pcs[t]):
                            es_ = layB[t][0][pi]
                            src, rb = halo_src(s, t, sub, c, kind, row0, "B")
                            copy_op("pool", r_(gx[sub][t][:, es_ - HB:es_]), src, reads=rb, writes=[gx_buf[sub][t]])
                            tt_op(r_(gx[sub][t][:, es_:es_ + ln]), psum[b1][:, tc0:tc0 + ln], tmpA[i][:, tc0:tc0 + ln],
                                  ALU.mult, reads=[psum_buf[b1], tmpA_buf[i]], writes=[gx_buf[sub][t]])
                        if t == 1 and s < NST - 1:
                            copy_op("pool", carryB[:, c, :], gx[sub][1][:, NT:NT + HB],
                                    reads=[gx_buf[sub][1]], writes=[c_carryB[c]])
                        if t == 1 and s == NST - 1:
                            copy_op("pool", gatB[:, :].rearrange("p (s w) -> p s w", w=HB),
                                    gx[sub][1][:, 144:144 + 3 * 94].rearrange("p (s w) -> p s w", w=94)[:, :, 0:HB],
                                    reads=[gx_buf[sub][1]], writes=[c_gatB])
                            b = bank()
                            transpose_group([(psum[b][0:3 * HB, 0:128], gatB[:, :], ident[:, :])],
                                            reads=[c_gatB, c_ident], wbuf=psum_buf[b])
                            copy_op("act", stage[0][0:3 * HB, c * 128:(c + 1) * 128], psum[b][0:3 * HB, 0:128],
                                    reads=[psum_buf[b]], writes=[stage_buf[0]])
            dummy_act(AF.Sqrt)
            conv_b(3)
            if s == NST - 1:
                out_toks.append(P.dma("sp", stage_st[0], ncb_p, stage[0][0:30, :], reads=[stage_buf[0]]))
                out_toks.append(P.dma("sp", stage_st[0], ncb_s, stage[0][30:90, :], reads=[stage_buf[0]]))
                out_toks.append(P.dma("sp", stage_st[1], nca_p, stage[1][0:2, :], reads=[stage_buf[1]]))
                out_toks.append(P.dma("sp", stage_st[1], nca_s, stage[1][2:6, :], reads=[stage_buf[1]]))

            lnb = []
            for t in range(2):
                bS1, bS2 = bank(), bank()
                for c in range(NCH):
                    i = tQ()
                    act_op(tmpQ[i][:, :], arena[:, 8 + c, tcols(t)], AF.Square,
                           reads=[a_buf[8 + c][t]], writes=[tmpQ_buf[i]])
                    mm_group(psum[bS1][:, 0:NT], [(ones[:, :], r_(arena[:, 8 + c, tcols(t)]))],
                             reads=[a_buf[8 + c][t], c_ones], wbuf=psum_buf[bS1], first=(c == 0), last=(c == NCH - 1))
                    mm_group(psum[bS2][:, 0:NT], [(ones[:, :], tmpQ[i][:, :])],
                             reads=[tmpQ_buf[i], c_ones], wbuf=psum_buf[bS2], first=(c == 0), last=(c == NCH - 1))
                lnb.append((bS1, bS2))
            lni = []
            for t in range(2):
                bS1, bS2 = lnb[t]
                im, iq, iv = tS(), tS(), tS()
                lni.append((im, iq, iv))
                ts_op(stt[im][:, :], psum[bS1][:, 0:NT], 1.0 / D, None, ALU.mult, ALU.bypass,
                      reads=[psum_buf[bS1]], writes=[stt_buf[im]])
                tt_op(stt[iq][:, :], stt[im][:, :], stt[im][:, :], ALU.mult, reads=[stt_buf[im]], writes=[stt_buf[iq]])
                stt_op(stt[iv][:, :], psum[bS2][:, 0:NT], 1.0 / D, stt[iq][:, :], ALU.mult, ALU.subtract,
                       reads=[psum_buf[bS2], stt_buf[iq]], writes=[stt_buf[iv]])
                ts_op(stt[iv][:, :], stt[iv][:, :], 0.0, None, ALU.max, ALU.bypass,
                      reads=[stt_buf[iv]], writes=[stt_buf[iv]])
            for t in range(2):
                im, iq, iv = lni[t]
                act_op(stt[iq][:, :], stt[iv][:, :], AF.Sqrt, reads=[stt_buf[iv], c_eps], writes=[stt_buf[iq]],
                       bias=epst[:, 0:1], scale=1.0)
            dummy_act(AF.Silu)

            def ln_centre(t):
                im, iq, iv = lni[t]
                for c in range(NCH):
                    tt_op(r_(arena[:, 8 + c, tcols(t)]), arena[:, 8 + c, tcols(t)], stt[im][:, :], ALU.subtract,
                          reads=[a_buf[8 + c][t], stt_buf[im]], writes=[a_buf[8 + c][t]], engine="dve")

            def ln_apply(t):
                im, iq, iv = lni[t]

                def fn(eng, iq=iq, iv=iv):
                    return eng.reciprocal(out=stt[iv][:, :], in_=stt[iq][:, :])
                P.op("dve", fn, reads=[stt_buf[iq]], writes=[stt_buf[iv]])
                for c in range(NCH):
                    j2 = tD()
                    stt_op(tmpD[j2][:, 0:NT], arena[:, 8 + c, tcols(t)], cvec[:, c, R_LNG:R_LNG + 1], stt[iv][:, :],
                           ALU.mult, ALU.mult, reads=[a_buf[8 + c][t], stt_buf[iv], c_cvec], writes=[tmpD_buf[j2]])
                    act_op(r_(arena[:, 8 + c, tcols(t)]), tmpD[j2][:, 0:NT], AF.Silu,
                           reads=[tmpD_buf[j2], c_cvec], writes=[a_buf[8 + c][t]], bias=cvec[:, c, R_LNB:R_LNB + 1])

            ln_centre(0)
            ln_apply(0)
            ln_centre(1)
            dummy_act(AF.Sigmoid)
            ln_pending = [lambda: ln_apply(1)]
            if not (KMAJOR and LN_SPLIT):
                ln_pending.pop()()

            for ff in range(4):
                slg = next_unit("GG")
                sla = next_unit("AO", hold=1)
                wg, wa = slot4(slg), slot4(sla)
                combos = [(sub, t) for sub in range(2) for t in range(2)]
                if ff == 0:
                    combos = [(sub, t) for t in range(2) for sub in range(2)]
                preM = {}

                def merge_pre(t):
                    bk = {}
                    hr = [h_buf[kb][t] for kb in range(NCH)]
                    for sub in range(2):
                        bGA, bGB, bYA = bank(), bank(), bank()
                        bk[("GA", sub)], bk[("GB", sub)], bk[("YA", sub)] = bGA, bGB, bYA
                        mm_group(psum[bGA][:, 0:NT], [(wg[:, 0, kb, sub * 128:(sub + 1) * 128], r_(hT[:, kb, tcols(t)]))
                                                      for kb in range(NCH)], reads=[slot_buf[slg]] + hr, wbuf=psum_buf[bGA])
                        mm_group(psum[bGB][:, 0:NT], [(wg[:, 1, kb, sub * 128:(sub + 1) * 128], r_(hT[:, kb, tcols(t)]))
                                                      for kb in range(NCH)], reads=[slot_buf[slg]] + hr, wbuf=psum_buf[bGB])
                        mm_group(psum[bYA][:, 0:NT], [(wa[:, 0, kb, sub * 128:(sub + 1) * 128], r_(arena[:, kb, tcols(t)]))
                                                      for kb in range(NCH)],
                                 reads=[slot_buf[sla]] + [a_buf[kb][t] for kb in range(NCH)], wbuf=psum_buf[bYA])
                    bk[("YB", 0)], bk[("YB", 1)] = bank(), bank()
                    for kb in range(NCH):
                        def fn(eng, kb=kb, bk=bk, t=t, wa=wa):
                            ins = None
                            for sub in range(2):
                                ins = eng.matmul(psum[bk[("YB", sub)]][:, 0:NT], lhsT=wa[:, 1, kb, sub * 128:(sub + 1) * 128],
                                                 rhs=r_(arena[:, 8 + kb, tcols(t)]), start=(kb == 0), stop=(kb == NCH - 1))
                            return ins
                        P.op("pe", fn, reads=[slot_buf[sla], a_buf[8 + kb][t]],
                             writes=[psum_buf[bk[("YB", 0)]], psum_buf[bk[("YB", 1)]]])
                    return bk

                for (sub, t) in combos:
                    f = 2 * ff + sub
                    if ff == 0 and KMAJOR:
                        if t not in preM:
                            if t == 1 and ln_pending:
                                dummy_act(AF.Silu)
                                ln_pending.pop()()
                                dummy_act(AF.Sigmoid)
                            preM[t] = merge_pre(t)
                        bYA, bYB, bGA, bGB = (preM[t][("YA", sub)], preM[t][("YB", sub)],
                                              preM[t][("GA", sub)], preM[t][("GB", sub)])
                    else:
                        bYA, bYB, bGA, bGB = bank(), bank(), bank(), bank()
                        hr = [h_buf[kb][t] for kb in range(NCH)]
                        mm_group(psum[bGA][:, 0:NT], [(wg[:, 0, kb, sub * 128:(sub + 1) * 128], r_(hT[:, kb, tcols(t)]))
                                                      for kb in range(NCH)], reads=[slot_buf[slg]] + hr, wbuf=psum_buf[bGA])
                        mm_group(psum[bGB][:, 0:NT], [(wg[:, 1, kb, sub * 128:(sub + 1) * 128], r_(hT[:, kb, tcols(t)]))
                                                      for kb in range(NCH)], reads=[slot_buf[slg]] + hr, wbuf=psum_buf[bGB])
                        mm_group(psum[bYA][:, 0:NT], [(wa[:, 0, kb, sub * 128:(sub + 1) * 128], r_(arena[:, kb, tcols(t)]))
                                                      for kb in range(NCH)],
                                 reads=[slot_buf[sla]] + [a_buf[kb][t] for kb in range(NCH)], wbuf=psum_buf[bYA])
                        mm_group(psum[bYB][:, 0:NT], [(wa[:, 1, kb, sub * 128:(sub + 1) * 128], r_(arena[:, 8 + kb, tcols(t)]))
                                                      for kb in range(NCH)],
                                 reads=[slot_buf[sla]] + [a_buf[8 + kb][t] for kb in range(NCH)], wbuf=psum_buf[bYB])
                    if True:
                        i1, i2 = tA(), tA()
                        act_op(tmpA[i1][:, 0:NT], psum[bGA][:, 0:NT], AF.Sigmoid, reads=[psum_buf[bGA]], writes=[tmpA_buf[i1]])
                        act_op(tmpA[i2][:, 0:NT], psum[bGB][:, 0:NT], AF.Sigmoid, reads=[psum_buf[bGB]], writes=[tmpA_buf[i2]])
                        j1, j2 = tD(), tD()
                        tt_op(tmpD[j1][:, 0:NT], psum[bYA][:, 0:NT], tmpA[i1][:, 0:NT], ALU.mult,
                              reads=[psum_buf[bYA], tmpA_buf[i1]], writes=[tmpD_buf[j1]])
                        tt_op(tmpD[j2][:, 0:NT], psum[bYB][:, 0:NT], tmpA[i2][:, 0:NT], ALU.mult,
                              reads=[psum_buf[bYB], tmpA_buf[i2]], writes=[tmpD_buf[j2]])
                        tt_op(r_(arena[:, 16 + f, tcols(t)]), tmpD[j1][:, 0:NT], tmpD[j2][:, 0:NT], ALU.add,
                              reads=[tmpD_buf[j1], tmpD_buf[j2]], writes=[a_buf[16 + f][t]])
            dummy_act(AF.Sqrt)
            for fp in range(2):
                sl = next_unit("WO")
                w4 = slot4(sl)
                if fp == 0 and KMAJOR:
                    def wo_first(w4, sl, t):
                        bk = {(g, sub): bank() for g in range(2) for sub in range(2)}
                        for kb in range(NCH):
                            def fn(eng, kb=kb, bk=bk, t=t):
                                ins = None
                                for g in range(2):
                                    for sub in range(2):
                                        ins = eng.matmul(psum[bk[(g, sub)]][:, 0:NT],
                                                         lhsT=w4[:, g, kb, sub * 128:(sub + 1) * 128],
                                                         rhs=r_(arena[:, 16 + kb, tcols(t)]),
                                                         start=(kb == 0), stop=(kb == NCH - 1))
                                return ins
                            P.op("pe", fn, reads=[slot_buf[sl], a_buf[16 + kb][t]],
                                 writes=[psum_buf[b_] for b_ in bk.values()])
                        for g in range(2):
                            for sub in range(2):
                                f = 2 * g + sub
                                bO = bk[(g, sub)]
                                tt_op(xT[:, f, tcols(t)], psum[bO][:, 0:NT], xT[:, f, tcols(t)], ALU.add,
                                      reads=[psum_buf[bO], x_buf[f][t]], writes=[x_buf[f][t]])
                    wo_first(w4, sl, 0)
                    wo_first(w4, sl, 1)
                    continue
                for g in range(2):
                    for sub in range(2):
                        f = 4 * fp + 2 * g + sub
                        for t in range(2):
                            bO = bank()
                            mm_group(psum[bO][:, 0:NT],
                                     [(w4[:, g, kb, sub * 128:(sub + 1) * 128], r_(arena[:, 16 + kb, tcols(t)]))
                                      for kb in range(NCH)],
                                     reads=[slot_buf[sl]] + [a_buf[16 + kb][t] for kb in range(NCH)], wbuf=psum_buf[bO])
                            tt_op(xT[:, f, tcols(t)], psum[bO][:, 0:NT], xT[:, f, tcols(t)], ALU.add,
                                  reads=[psum_buf[bO], x_buf[f][t]], writes=[x_buf[f][t]])

        def store_y(s):
            for (kind, r0, nb, c0) in io_blocks(s):
                sg = rr("stage", 2)
                for g in range(2):
                    b = bank()
                    rb = [x_buf[4 * g + i][t] for i in range(4) for t in tiles_of(c0, nb)]
                    transpose_group([(psum[b][0:nb, i * 128:(i + 1) * 128], xT[:, 4 * g + i, c0:c0 + nb], ident[:, :])
                                     for i in range(4)], reads=rb + [c_ident], wbuf=psum_buf[b])
                    copy_op("act" if g == 0 else "dve", stage[sg][0:nb, g * 512:(g + 1) * 512], psum[b][0:nb, :],
                            reads=[psum_buf[b]], writes=[stage_buf[sg]])
                if kind == "P":
                    dst = y_p[r0:r0 + nb, :]
                else:
                    o = 0 if kind == "S0" else 64
                    dst = y_s[o + r0:o + r0 + nb, :]
                out_toks.append(P.dma("sp", stage_st[sg], dst, stage[sg][0:nb, :], reads=[stage_buf[sg]]))

        def dump():
            allb = [b_ for l in x_buf for b_ in l] + [b_ for l in h_buf for b_ in l] + [b_ for l in a_buf for b_ in l]
            out_toks.append(P.dma("sp", misc_sem, dbg_x, xT[:, :, :].rearrange("p c t -> p (c t)"), reads=allb))
            out_toks.append(P.dma("sp", misc_sem, dbg_h, hT[:, :, :].rearrange("p c t -> p (c t)"), reads=allb))
            out_toks.append(P.dma("sp", misc_sem, dbg_a, arena[:, :, :].rearrange("p c t -> p (c t)"), reads=allb))

        phases = []
        for s in range(NST):
            phases += [("load", lambda s=s: load_x(s, prefetched=((0, 1, 2) if (s > 0 and dbg is None) else ()))),
                       ("norm1", lambda: rms_norm(R_FFN1, False, AF.Silu)),
                       ("ffn1", lambda: ffn(0)), ("norm2", lambda: rms_norm(R_MIX, False, AF.Sigmoid)),
                       ("mixer", lambda s=s: mixer(s)), ("norm3", lambda: rms_norm(R_FFN2, False, AF.Silu)),
                       ("ffn2", lambda s=s: ffn(1, after_gu=((lambda: prefetch_x(s + 1))
                                                             if (s + 1 < NST and dbg is None) else None))),
                       ("norm4", lambda s=s: (rms_norm(R_FIN, True),
                                              prefetch_x(s + 1, ks=(2,)) if (s + 1 < NST and dbg is None) else None)),
                       ("store", lambda s=s: store_y(s))]
        for pi, (name, fn_) in enumerate(phases):
            fn_()
            if dbg is not None and dbg == (pi // 9, name):
                dump()
                break
        if dbg is None:
            assert ustate["next"] == len(units)
        P.wait_all("sp", out_toks)

        with nc.Block() as block:
            @block.sync
            def _(eng):
                for f in P.streams["sp"]:
                    f(eng)

            @block.gpsimd
            def _(eng):
                for f in P.streams["pool"]:
                    f(eng)

            @block.tensor
            def _(eng):
                for f in P.streams["pe"]:
                    f(eng)

            @block.vector
            def _(eng):
                for f in P.streams["dve"]:
                    f(eng)

            @block.scalar
            def _(eng):
                for f in P.streams["act"]:
                    f(eng)
    return nc


_NC_CACHE = {}


def kernel(x_prompt, x_sample, cache_conv_a, cache_conv_b, ffn1_norm, ffn1_w_gate_up, ffn1_w_down, mix_norm,
           w_in, conv_a_w, conv_b_w, conv_b_bias, conv_b_ln_g, conv_b_ln_b, w_a_out, w_b_out, w_out,
           ffn2_norm, ffn2_w_gate_up, ffn2_w_down, final_norm):
    f32 = np.float32
    A = lambda a: np.ascontiguousarray(np.asarray(a, dtype=f32))
    n = 8
    if "nc" not in _NC_CACHE:
        _NC_CACHE["nc"] = build_nc()
    nc = _NC_CACHE["nc"]
    cst = np.concatenate([A(ffn1_norm).reshape(1, D), A(mix_norm).reshape(1, D), A(ffn2_norm).reshape(1, D),
                          A(final_norm).reshape(1, D), A(conv_b_bias).reshape(1, D), A(conv_b_ln_g).reshape(1, D),
                          A(conv_b_ln_b).reshape(1, D), A(conv_a_w).reshape(3, D), A(conv_b_w).reshape(31, D)], axis=0)
    shared = {
        "cst": np.ascontiguousarray(cst),
        "ident_in": np.eye(128, dtype=f32),
        "w_gu1": A(ffn1_w_gate_up)[0], "w_gu2": A(ffn2_w_gate_up)[0],
        "w_dn1": A(ffn1_w_down)[0], "w_dn2": A(ffn2_w_down)[0],
        "w_in": A(w_in)[0], "w_ao": A(w_a_out)[0], "w_bo": A(w_b_out)[0], "w_o": A(w_out)[0],
    }
    xp, xs = A(x_prompt), A(x_sample)
    ca, cb = A(cache_conv_a)[0], A(cache_conv_b)[0]
    in_maps = []
    for i in range(n):
        m = dict(shared)
        m["x_p"] = xp[i]
        m["x_s"] = np.ascontiguousarray(xs[2 * i:2 * i + 2].reshape(128, D))
        m["hist"] = np.ascontiguousarray(np.concatenate([cb[2 * i], cb[2 * i + 1], ca[2 * i], ca[2 * i + 1]], axis=0))
        in_maps.append(m)
    res = run_bass_kernel_spmd(nc, in_maps, core_ids=list(range(n)))
    R = res.results
    y_prompt = np.stack([R[i]["y_p"] for i in range(n)], axis=0).astype(f32)
    y_sample = np.concatenate([R[i]["y_s"].reshape(2, 64, D) for i in range(n)], axis=0).astype(f32)
    nca_p = np.stack([R[i]["nca_p"] for i in range(n)], axis=0)[None].astype(f32)
    ncb_p = np.stack([R[i]["ncb_p"] for i in range(n)], axis=0)[None].astype(f32)
    nca_s = np.concatenate([R[i]["nca_s"].reshape(2, 2, D) for i in range(n)], axis=0)[None].astype(f32)
    ncb_s = np.concatenate([R[i]["ncb_s"].reshape(2, 30, D) for i in range(n)], axis=0)[None].astype(f32)
    return (y_prompt, y_sample, nca_p, ncb_p, nca_s, ncb_s)
```
